# Optimizing a Trainium2 kernel written in Bass

```python
import jax, jax.numpy as jnp
from jax import lax
import numpy as np

D_MODEL = 1024
BATCH = 16
SEQ = 2048
DEPTH = 1
DEC_BATCH = 128
DEC_SEQ = 8
PAST_LEN = 8192
PAGE_SIZE = 128

D_MIX = D_MODEL
D_A = D_MIX // 2
N_GROUPS_A = 4
GROUP_A = D_A // N_GROUPS_A
CHUNK = 128
D_B = D_MIX - D_A
N_HEADS_B = 8
HEAD_DIM = D_B // N_HEADS_B
PATTERNS = ((128, 1), (512, 4), (2048, 16))
WINDOW_MAX = 2048
BLK = 128
D_PLE = 256
EPS = 1e-6
NEG = -1e30
SCALE = HEAD_DIM ** -0.5
SPLITS = [D_A, 2 * D_A, 3 * D_A, 3 * D_A + D_B, 3 * D_A + 2 * D_B, 3 * D_A + 3 * D_B]
D_IN = 3 * D_A + 4 * D_B

kernel_name = "hymba_gmlp_dilated_attn_step"


def rms_norm(x, g):
    xf = x.astype(jnp.float32)
    y = xf * lax.rsqrt(jnp.mean(xf * xf, axis=-1, keepdims=True) + EPS)
    return (y * g.astype(jnp.float32)).astype(x.dtype)


def project(x, g_norm, w_in, g_q, g_k):
    B, S, _ = x.shape
    h = rms_norm(x, g_norm)
    proj = jnp.einsum('bsd,de->bse', h, w_in)
    u, va, za, q, k, vb, zb = jnp.split(proj, SPLITS, axis=-1)
    q = rms_norm(q.reshape(B, S, N_HEADS_B, HEAD_DIM), g_q)
    k = rms_norm(k.reshape(B, S, N_HEADS_B, HEAD_DIM), g_k)
    vb = vb.reshape(B, S, N_HEADS_B, HEAD_DIM)
    return u, va, za, q, k, vb, zb


def chunk_mlp(u, v, z, w_s, b_s, g_va, g_oa):
    B, L, _ = u.shape
    c = min(L, CHUNK)
    vn = rms_norm(v.reshape(B, L, N_GROUPS_A, GROUP_A), g_va)
    vc = vn.reshape(B, L // c, c, N_GROUPS_A, GROUP_A)
    ws = jnp.tril(w_s[:, :c, :c])
    mixed = jnp.einsum('gts,bnsgc->bntgc', ws, vc) + b_s[:, :c].T[None, None, :, :, None]
    a = u * mixed.reshape(B, L, D_A)
    a = rms_norm(a.reshape(B, L, N_GROUPS_A, GROUP_A), g_oa).reshape(B, L, D_A)
    return a * jax.nn.silu(z), vn.reshape(B, L, D_A)


def dilated_prompt_pattern(q, k, v, n, r):
    B, S, H, Dh = q.shape
    L = S // r
    nb = -(-L // BLK)
    Lp = nb * BLK

    def to_class(x):
        x = x.reshape(B, L, r, H, Dh).transpose(0, 2, 1, 3, 4)
        return jnp.pad(x, ((0, 0), (0, 0), (0, Lp - L), (0, 0), (0, 0)))

    def band(x):
        prev = jnp.pad(x, ((0, 0), (0, 0), (BLK, 0), (0, 0), (0, 0)))[:, :, :Lp]
        return jnp.concatenate([prev.reshape(B, r, nb, BLK, H, Dh),
                                x.reshape(B, r, nb, BLK, H, Dh)], axis=3)

    qb = to_class(q).reshape(B, r, nb, BLK, H, Dh).astype(jnp.float32)
    kb = band(to_class(k)).astype(jnp.float32)
    vb = band(to_class(v)).astype(jnp.float32)
    s = jnp.einsum('brnqhd,brnkhd->brnhqk', qb, kb) * SCALE
    qi = jnp.arange(BLK)[:, None]
    ki = jnp.arange(2 * BLK)[None, :]
    dist = BLK + qi - ki
    valid = ((dist >= 0) & (dist <= n))[None] & (
        (jnp.arange(nb)[:, None, None] > 0) | (ki >= BLK)[None])
    s = jnp.where(valid[None, None, :, None], s, NEG)
    m = jnp.max(s, axis=-1, keepdims=True)
    p = jnp.exp(s - m)
    den = jnp.sum(p, axis=-1, keepdims=True)
    o = jnp.einsum('brnhqk,brnkhd->brnhqd', p, vb) / den
    lse = (m + jnp.log(den))[..., 0]
    o = o.transpose(0, 1, 2, 4, 3, 5).reshape(B, r, Lp, H, Dh)[:, :, :L]
    o = o.transpose(0, 2, 1, 3, 4).reshape(B, S, H, Dh)
    lse = lse.transpose(0, 1, 2, 4, 3).reshape(B, r, Lp, H)[:, :, :L]
    lse = lse.transpose(0, 2, 1, 3).reshape(B, S, H)
    return o, lse


def dilated_sample_pattern(q, k_all, v_all, n, r):
    B, T, H, Dh = q.shape
    wb = k_all.shape[1] - T
    idx = wb + jnp.arange(T)[:, None] - r * jnp.arange(n + 1)[None, :]
    valid = idx >= 0
    idx = jnp.maximum(idx, 0)
    kg = jnp.take(k_all, idx, axis=1).astype(jnp.float32)
    vg = jnp.take(v_all, idx, axis=1).astype(jnp.float32)
    s = jnp.einsum('bthd,btkhd->bthk', q.astype(jnp.float32), kg) * SCALE
    s = jnp.where(valid[None, :, None, :], s, NEG)
    m = jnp.max(s, axis=-1, keepdims=True)
    p = jnp.exp(s - m)
    den = jnp.sum(p, axis=-1, keepdims=True)
    o = jnp.einsum('bthk,btkhd->bthd', p, vg) / den
    return o, (m + jnp.log(den))[..., 0]


def combine_by_denominator(outs, lses):
    w = jax.nn.softmax(jnp.stack(lses, axis=0), axis=0)
    return jnp.einsum('pbsh,pbshd->bshd', w, jnp.stack(outs, axis=0))


def finish(x, a_out, b_out, zb, g_ob, w_out, p, w_ple, g_ple, w_ple_gate):
    B, S, _ = x.shape
    b = rms_norm(b_out.astype(x.dtype), g_ob).reshape(B, S, D_B) * jax.nn.silu(zb)
    h = x + jnp.einsum('bse,ed->bsd', jnp.concatenate([a_out, b], axis=-1), w_out)
    gate = jax.nn.sigmoid(jnp.einsum('bsd,de->bse', h, w_ple_gate))
    e = rms_norm(jnp.einsum('bsp,pd->bsd', p, w_ple), g_ple)
    return h + gate * e


def setup_inputs(seed: int = 0) -> dict:
    key = jax.random.key(seed)
    ks = jax.random.split(key, 24)
    wb = min(WINDOW_MAX, PAST_LEN)
    f = jnp.float32

    def nrm(k, shape, scale=1.0):
        return jax.random.normal(k, shape, f) * scale

    def gain(k, shape):
        return 1.0 + 0.02 * jax.random.normal(k, shape, f)

    return {
        "x_prompt": nrm(ks[0], (BATCH, SEQ, D_MODEL)),
        "x_sample": nrm(ks[1], (DEC_BATCH, DEC_SEQ, D_MODEL)),
        "cache_k": nrm(ks[2], (DEPTH, DEC_BATCH, wb, N_HEADS_B, HEAD_DIM)),
        "cache_v": nrm(ks[3], (DEPTH, DEC_BATCH, wb, N_HEADS_B, HEAD_DIM)),
        "p_prompt": nrm(ks[4], (DEPTH, BATCH, SEQ, D_PLE)),
        "p_sample": nrm(ks[5], (DEPTH, DEC_BATCH, DEC_SEQ, D_PLE)),
        "g_norm": gain(ks[6], (DEPTH, D_MODEL)),
        "w_in": nrm(ks[7], (DEPTH, D_MODEL, D_IN), D_MODEL ** -0.5),
        "w_s": nrm(ks[8], (DEPTH, N_GROUPS_A, CHUNK, CHUNK), CHUNK ** -0.5),
        "b_s": 1.0 + 0.02 * nrm(ks[9], (DEPTH, N_GROUPS_A, CHUNK)),
        "g_va": gain(ks[10], (DEPTH, N_GROUPS_A, GROUP_A)),
        "g_oa": gain(ks[11], (DEPTH, N_GROUPS_A, GROUP_A)),
        "g_q": gain(ks[12], (DEPTH, HEAD_DIM)),
        "g_k": gain(ks[13], (DEPTH, HEAD_DIM)),
        "g_ob": gain(ks[14], (DEPTH, N_HEADS_B, HEAD_DIM)),
        "w_out": nrm(ks[15], (DEPTH, D_MIX, D_MODEL), D_MIX ** -0.5),
        "w_ple": nrm(ks[16], (DEPTH, D_PLE, D_MODEL), D_PLE ** -0.5),
        "g_ple": gain(ks[17], (DEPTH, D_MODEL)),
        "w_ple_gate": nrm(ks[18], (DEPTH, D_MODEL, D_MODEL), D_MODEL ** -0.5),
    }


def reference(x_prompt, x_sample, cache_k, cache_v, p_prompt, p_sample, g_norm, w_in, w_s,
              b_s, g_va, g_oa, g_q, g_k, g_ob, w_out, w_ple, g_ple, w_ple_gate):
    y_p, y_s = x_prompt, x_sample
    kp_list, vp_list, ks_list, vs_list, va_list = [], [], [], [], []
    for i in range(DEPTH):
        u, va, za, q, k, vb, zb = project(y_p, g_norm[i], w_in[i], g_q[i], g_k[i])
        a_out, _ = chunk_mlp(u, va, za, w_s[i], b_s[i], g_va[i], g_oa[i])
        outs, lses = [], []
        for (w, r) in PATTERNS:
            o, l = dilated_prompt_pattern(q, k, vb, w // r, r)
            outs.append(o)
            lses.append(l)
        b_out = combine_by_denominator(outs, lses)
        wb_p = min(WINDOW_MAX, y_p.shape[1])
        kp_list.append(k[:, -wb_p:])
        vp_list.append(vb[:, -wb_p:])
        y_p = finish(y_p, a_out, b_out, zb, g_ob[i], w_out[i], p_prompt[i], w_ple[i],
                     g_ple[i], w_ple_gate[i])

        u, va, za, q, k, vb, zb = project(y_s, g_norm[i], w_in[i], g_q[i], g_k[i])
        a_out, va_rows = chunk_mlp(u, va, za, w_s[i], b_s[i], g_va[i], g_oa[i])
        k_all = jnp.concatenate([cache_k[i].astype(k.dtype), k], axis=1)
        v_all = jnp.concatenate([cache_v[i].astype(vb.dtype), vb], axis=1)
        outs, lses = [], []
        for (w, r) in PATTERNS:
            o, l = dilated_sample_pattern(q, k_all, v_all, w // r, r)
            outs.append(o)
            lses.append(l)
        b_out = combine_by_denominator(outs, lses)
        ks_list.append(k)
        vs_list.append(vb)
        va_list.append(va_rows)
        y_s = finish(y_s, a_out, b_out, zb, g_ob[i], w_out[i], p_sample[i], w_ple[i],
                     g_ple[i], w_ple_gate[i])

    k_win_prompt = jnp.stack(kp_list, axis=0)
    v_win_prompt = jnp.stack(vp_list, axis=0)
    k_new_sample = jnp.stack(ks_list, axis=0)
    v_new_sample = jnp.stack(vs_list, axis=0)
    va_chunk_sample = jnp.stack(va_list, axis=0)
    return (y_p, y_s, k_win_prompt, v_win_prompt, k_new_sample, v_new_sample, va_chunk_sample)
```

```python
import sys
import numpy as np
from contextlib import ExitStack
import concourse.bass as bass
import concourse.mybir as mybir
from concourse.bass_utils import run_bass_kernel_spmd

F32 = mybir.dt.float32
BF16 = mybir.dt.bfloat16
AF = mybir.ActivationFunctionType
ALU = mybir.AluOpType
AX = mybir.AxisListType

EPS = 1e-6
SCALE = 64 ** -0.5
NCORES = 8
NSEQ = 2
NJB = 4
NB_S = 16
WB = 2048


class Sem:
    def __init__(self, h, step):
        self.h = h
        self.step = step
        self.count = 0
        self.queue = None


class Buf:
    def __init__(self, name):
        self.name = name
        self.w = []
        self.r = []


class OpRec:
    __slots__ = ("idx", "eng", "fn", "deps", "dur", "dma_sem", "nbytes", "ev", "seg", "t0", "t1", "line")


SEM_LAT = 0.25
DMA_LAT = 1.8
DMA_BW = 230e3
DMA_ISSUE = {"sp": 0.35, "pool": 0.65}
WINDOW = 40
PRIO = 1
SLACK = 0.3
ENGS = ("pe", "act", "dve", "pool", "sp")


class Prog:
    def __init__(self, nc, stack):
        self.nc = nc
        self.stack = stack
        self.nsem = 0
        self.dsems = []
        self.esem = {}
        for n in ENGS:
            self.esem[n] = self.new_sem(n, 1)
        self.ops = []
        self.seg = 0
        self.seg_start = [0]
        self.marks = []

    def new_sem(self, name, step=16):
        self.nsem += 1
        h = self.stack.enter_context(self.nc.semaphore("s_%s_%d" % (name, self.nsem)))
        s = Sem(h, step)
        if step == 16:
            self.dsems.append(s)
        return s

    def sb(self, name, shape, dtype):
        return self.stack.enter_context(self.nc.sbuf_tensor("sb_" + name, list(shape), dtype))

    def ps(self, name, shape, dtype):
        return self.stack.enter_context(self.nc.psum_tensor("ps_" + name, list(shape), dtype))

    def _record(self, engname, fn, reads, writes, dur, dma_sem=None, nbytes=0, par=False):
        o = OpRec()
        o.idx = len(self.ops)
        o.eng = engname
        o.fn = fn
        o.dur = dur
        o.dma_sem = dma_sem
        o.nbytes = nbytes
        o.seg = self.seg
        o.ev = None
        f = sys._getframe(2)
        ln = []
        while f is not None and len(ln) < 4:
            if f.f_code.co_name not in ("ACT", "TT", "TS", "CP", "MEMSET", "MMS", "TRS", "op", "dma"):
                ln.append(f.f_lineno)
            f = f.f_back
        o.line = ln
        lo = self.seg_start[-1]
        deps = {}
        for b in reads:
            for w in b.w:
                if w >= lo:
                    deps[w] = True
        for b in writes:
            if not par:
                for w in b.w:
                    if w >= lo:
                        deps.setdefault(w, False)
            for r in b.r:
                if r >= lo:
                    deps.setdefault(r, False)
        o.deps = deps
        for b in reads:
            b.r.append(o.idx)
        for b in writes:
            if par:
                b.w.append(o.idx)
            else:
                b.w = [o.idx]
                b.r = []
        self.ops.append(o)
        return o

    def op(self, engname, fn, reads=(), writes=(), dur=0.5):
        return self._record(engname, fn, reads, writes, dur)

    def dma(self, engname, out, in_, sem, reads=(), writes=(), nbytes=65536, par=False):
        assert sem.queue in (None, engname)
        sem.queue = engname
        return self._record(engname, lambda e: e.dma_start(out=out, in_=in_), reads, writes,
                            DMA_ISSUE[engname], dma_sem=sem, nbytes=nbytes, par=par)

    def mark(self, name):
        self.marks.append((name, len(self.ops)))

    def barrier(self):
        self.seg += 1
        self.seg_start.append(len(self.ops))

    def _schedule(self, ops, t0):
        pending = {e: [] for e in ENGS}
        for o in ops:
            pending[o.eng].append(o)
        tail = {}
        succ = {}
        for o in ops:
            for d in o.deps:
                succ.setdefault(d, []).append(o)
        for o in reversed(ops):
            t = 0.0
            for s_ in succ.get(o.idx, ()):
                if tail[s_.idx] > t:
                    t = tail[s_.idx]
            tail[o.idx] = t + o.dur + (DMA_LAT + o.nbytes / DMA_BW if o.dma_sem is not None else 0.0)
        done = {}
        order = {e: [] for e in ENGS}
        eng_time = {e: t0 for e in ENGS}
        dma_free = t0
        remaining = len(ops)
        allops = self.ops
        tmax = t0
        while remaining:
            best = None
            for e in ENGS:
                et = eng_time[e]
                cnt = 0
                for o in pending[e]:
                    cnt += 1
                    if cnt > WINDOW:
                        break
                    ready = et
                    ok = True
                    for d in o.deps:
                        td = done.get(d)
                        if td is None:
                            ok = False
                            break
                        if allops[d].eng != e or allops[d].dma_sem is not None:
                            td += SEM_LAT
                        if td > ready:
                            ready = td
                    if not ok:
                        continue
                    if PRIO == 0:
                        key = (ready, o.idx)
                    else:
                        key = (max(ready, et + SLACK) if ready <= et + SLACK else ready, -tail[o.idx])
                    if best is None or key < best[0]:
                        best = (key, e, o, ready)
                    if PRIO == 0 and ready <= et:
                        break
            _, e, o, start = best
            pending[e].remove(o)
            order[e].append(o)
            if o.dma_sem is not None:
                iend = start + o.dur
                ts = max(iend, dma_free)
                dma_free = ts + o.nbytes / DMA_BW
                fin = dma_free + DMA_LAT
                eng_time[e] = iend
            else:
                fin = start + o.dur
                eng_time[e] = fin
            done[o.idx] = fin
            o.t0 = start
            o.t1 = fin
            if fin > tmax:
                tmax = fin
            remaining -= 1
        return order, tmax

    def emit(self):
        nc = self.nc
        nseg = self.seg + 1
        bounds = self.seg_start + [len(self.ops)]
        queues = {e: [] for e in ENGS}
        t = 0.0
        for s in range(nseg):
            ops = self.ops[bounds[s]:bounds[s + 1]]
            order, t = self._schedule(ops, t)
            for e in ENGS:
                for o in order[e]:
                    queues[e].append(o)
                queues[e].append(None)
        self.sim_time = t
        pos = {e: 0 for e in ENGS}
        for e in ENGS:
            for o in queues[e]:
                if o is None:
                    continue
                if o.dma_sem is not None:
                    o.dma_sem.count += 16
                    o.ev = (o.dma_sem, o.dma_sem.count)
                else:
                    pos[e] += 1
                    o.ev = (self.esem[e], pos[e])
        bar_targets = []
        cur = {}
        segops = [[] for _ in range(nseg)]
        for o in self.ops:
            segops[o.seg].append(o)
        for s in range(nseg):
            for o in segops[s]:
                S, v = o.ev
                if cur.get(S, 0) < v:
                    cur[S] = v
            bar_targets.append(dict(cur))
        prog = {}
        for e in ENGS:
            seen = {}
            out = []
            s = 0
            for o in queues[e]:
                if o is None:
                    waits = []
                    for S, v in bar_targets[s].items():
                        if S is self.esem[e] or seen.get(S, 0) >= v:
                            continue
                        seen[S] = v
                        waits.append((S, v))
                    out.append((waits, None, None, 0))
                    s += 1
                    continue
                waits = []
                for d, raw in o.deps.items():
                    dop = self.ops[d]
                    S, v = dop.ev
                    if dop.eng == e and dop.dma_sem is None:
                        if e == "pe" or not raw:
                            continue
                    if seen.get(S, 0) >= v:
                        continue
                    seen[S] = v
                    waits.append((S, v))
                S, v = o.ev
                out.append((waits, o.fn, S, S.step))
            prog[e] = out
        with nc.Block() as block:
            def run(lst):
                def body(e):
                    for waits, fn, sem, inc in lst:
                        for S, v in waits:
                            e.wait_ge(S.h, v)
                        if fn is not None:
                            ins = fn(e)
                            ins.then_inc(sem.h, inc)
                return body
            block.tensor(run(prog["pe"]))
            block.scalar(run(prog["act"]))
            block.vector(run(prog["dve"]))
            block.gpsimd(run(prog["pool"]))
            block.sync(run(prog["sp"]))


def build_nc():
    nc = bass.Bass("TRN2", target_bir_lowering=False)

    def din(name, shape, dt=F32):
        return nc.dram_tensor(name, list(shape), dt, kind="ExternalInput").ap()

    def dout(name, shape):
        return nc.dram_tensor(name, list(shape), F32, kind="ExternalOutput").ap()

    xp = din("xp", [4096, 1024]); pp = din("pp", [4096, 256])
    xs = din("xs", [128, 1024]); ps_ = din("ps", [128, 256])
    ck = din("ck", [16, 2048, 512]); cv = din("cv", [16, 2048, 512])
    w_in = din("w_in", [1024, 3584]); w_out = din("w_out", [1024, 1024])
    w_gate = din("w_gate", [1024, 1024]); w_ple = din("w_ple", [256, 1024])
    gn_col = din("gn_col", [128, 8])
    wsT_d = din("wsT", [4, 128, 128]); wsS_d = din("wsS", [4, 128, 128])
    bcol_d = din("bcol", [128, 8])
    gvec = din("gvec", [4, 512])
    gple_d = din("gple", [1, 1024])
    gcol_d = din("gcol", [128, 1])
    gocol_d = din("gocol", [128, 8])
    ident_d = din("ident", [128, 128])
    masks_d = din("masks", [5, 128, 128])
    maskc_d = din("maskc", [128, 10 * 64])
    maskn_d = din("maskn", [128, 1024])

    yp = dout("yp", [4096, 1024]); ys = dout("ys", [128, 1024])
    kw = dout("kw", [4096, 512]); vw = dout("vw", [4096, 512])
    kn = dout("kn", [128, 512]); vn_o = dout("vn", [128, 512]); vac = dout("vac", [128, 512])
    vscr = nc.dram_tensor("vscr", [4096, 520], BF16, kind="Internal").ap()

    with ExitStack() as st:
        P = Prog(nc, st)

        def edur(eng, out, *ins):
            n = out.free_size()
            if eng == "act":
                return 0.2 + n / 1000.0
            if eng == "pool":
                return 0.5 + n / 350.0
            fast = out.dtype == BF16 and all(getattr(i, "dtype", BF16) == BF16 for i in ins)
            return 0.1 + n / (1900.0 if fast else 940.0)

        def ACT(out, in_, func, R, W, scale=1.0, accum=None):
            d = edur("act", out)
            if accum is None:
                P.op("act", lambda e: e.activation(out=out, in_=in_, func=func, scale=scale), R, W, d)
            else:
                P.op("act", lambda e: e.activation(out=out, in_=in_, func=func, scale=scale, accum_out=accum), R, W, d + 0.1)

        def TT(eng, out, in0, in1, op, R, W):
            d = 1.2 if (eng == "pool" and op == ALU.pow) else edur(eng, out, in0, in1)
            P.op(eng, lambda e: e.tensor_tensor(out, in0, in1, op), R, W, d)

        def TS(eng, out, in0, s1, s2, op0, op1, R, W):
            d = edur(eng, out, in0)
            if s2 is None:
                P.op(eng, lambda e: e.tensor_scalar(out, in0, s1, None, op0), R, W, d)
            else:
                P.op(eng, lambda e: e.tensor_scalar(out, in0, s1, s2, op0, op1), R, W, d)

        def CP(eng, out, in_, R, W):
            d = edur(eng, out, in_)
            if eng == "act":
                P.op("act", lambda e: e.copy(out=out, in_=in_), R, W, d)
            else:
                P.op(eng, lambda e: e.tensor_copy(out, in_), R, W, d)

        def MEMSET(eng, ap, val, W):
            P.op(eng, lambda e: e.memset(ap, val), (), W, edur(eng, ap))

        def MMS(items, R, W):
            def fn(e):
                ins = None
                for (o, l, r, s0, s1) in items:
                    ins = e.matmul(o, l, r, start=s0, stop=s1, skip_group_check=True)
                return ins
            d = sum(0.062 + r.free_size() / 3200.0 for (o, l, r, s0, s1) in items)
            P.op("pe", fn, R, W, d)

        def TRS(items, R, W):
            def fn(e):
                ins = None
                for (o, i, idn) in items:
                    ins = e.transpose(o, i, idn)
                return ins
            d = sum((0.25 if i.dtype == F32 else 0.11) for (o, i, idn) in items)
            P.op("pe", fn, R, W, d)

        win = P.sb("win", [128, 8, 3584], BF16); Bwin = Buf("win")
        wout = P.sb("wout", [128, 8, 1024], BF16); Bwout = Buf("wout")
        wgate = P.sb("wgate", [128, 8, 1024], BF16); Bwgate = Buf("wgate")
        wple = P.sb("wple", [128, 2, 1024], BF16); Bwple = Buf("wple")
        big = P.sb("big", [128, 8192 + 16 * 520], BF16); Bbig = Buf("big")
        kT = big[:, 0:8192].rearrange("p (c t) -> p c t", c=4)
        vr4 = big[:, 8192:8192 + 16 * 520].rearrange("p (n f) -> p n f", n=16)
        BkT = [Buf("kT%d" % i) for i in range(4)]
        Bvr4 = [Buf("vr4_%d" % i) for i in range(4)]
        qT = P.sb("qT", [128, 4, 512], BF16); BqT = Buf("qT")
        vnat = P.sb("vnat", [128, 5, 520], BF16); Bvnat = [Buf("vnat%d" % i) for i in range(5)]
        aout = P.sb("aout", [128, 4, 512], BF16); Baout = [Buf("aout%d" % i) for i in range(4)]
        zbg = P.sb("zbg", [128, 4, 512], BF16); Bzbg = [Buf("zbg%d" % i) for i in range(4)]
        oacc = P.sb("oacc", [128, 8, 512], F32); Boacc = [Buf("oacc%d" % i) for i in range(8)]
        identf = P.sb("identf", [128, 128], F32); identb = P.sb("identb", [128, 128], BF16); Bid = Buf("ident")
        masks = P.sb("masks", [128, 5, 128], BF16); Bmask = Buf("masks")
        gbc = P.sb("gbc", [128, 1, 512], F32); Bgbc = Buf("gbc")
        gk64 = P.sb("gk64", [128, 64], F32)
        gple = P.sb("gple", [128, 1024], F32); Bgple = Buf("gple")
        cols = P.sb("cols", [128, 40], F32); Bcols = Buf("cols")
        wsb = P.sb("wsb", [128, 8, 128], BF16); Bws = Buf("ws")
        xinA = P.sb("xinA", [128, 1024], F32); BxinA = Buf("xinA")
        b1A = P.sb("b1A", [128, 1024], BF16); Bb1A = Buf("b1A")
        wsf = xinA[:].rearrange("p (a b) -> p a b", a=8)
        stg = [oacc[:, 2 * i:2 * i + 2, :].rearrange("p a b -> p (a b)") for i in range(4)]
        Bstg = [[Boacc[2 * i], Boacc[2 * i + 1]] for i in range(4)]
        mstage = stg[0][:, 0:640].rearrange("p (a b) -> p a b", a=5)
        b1 = P.sb("b1", [128, 1024], BF16); Bb1 = Buf("b1")
        tT = [P.sb("tT%d" % i, [128, 8, 128], BF16) for i in range(2)]; BtT = [Buf("tT%d" % i) for i in range(2)]
        wk = P.sb("wk", [128, 7, 512], F32); Bwk = [Buf("wk%d" % i) for i in range(7)]
        wc = P.sb("wc", [128, 5, 512], F32); Bwc = [Buf("wc%d" % i) for i in range(5)]
        vnb = P.sb("vnb", [128, 512], BF16); Bvnb = Buf("vnb")
        qb = vnb; Bqb = Bvnb
        kb = vnb; Bkb = Bvnb
        NPT = 6
        pT = [wk[:, 6 - i, :].bitcast(BF16) for i in range(NPT)]; BpT = [Bwk[6 - i] for i in range(NPT)]
        pin = P.sb("pin", [128, 256], F32); Bpin = Buf("pin")
        pbf = P.sb("pbf", [128, 256], BF16); Bpbf = Buf("pbf")
        pTt = P.sb("pTt", [128, 2, 128], BF16); BpTt = Buf("pTt")
        stx2 = P.sb("stx", [128, 2, 4], F32); Bstx2 = [Buf("stx0"), Buf("stx1")]
        stv = P.sb("stv", [128, 12], F32); Bstv = Buf("stv")
        stq = P.sb("stq", [128, 24], F32); Bstq = Buf("stq")
        stk = P.sb("stk", [128, 24], F32); Bstk = Buf("stk")
        sta = P.sb("sta", [128, 12], F32); Bsta = Buf("sta")
        stb = P.sb("stb", [128, 32], F32); Bstb = Buf("stb")
        ste = P.sb("ste", [128, 8], F32); Bste = Buf("ste")

        ptr = [P.ps("ptr%d" % i, [128, 1024], BF16) for i in range(2)]; Bptr = [Buf("ptr%d" % i) for i in range(2)]
        pm = [P.ps("pm%d" % i, [128, 512], F32) for i in range(4)]; Bpm = [Buf("pm%d" % i) for i in range(4)]
        pa = [P.ps("pa%d" % i, [128, 512], F32) for i in range(2)]; Bpa = [Buf("pa%d" % i) for i in range(2)]

        cnt = {"pm": 0, "ptr": 0, "stg": 0, "pT": 0, "stx": 0, "pS": 0}

        def nxt(k, n):
            v = cnt[k] % n
            cnt[k] += 1
            return v

        S_x = [P.new_sem("x%d" % i) for i in range(2)]
        S_c = P.new_sem("c")
        S_kc = [P.new_sem("kc%d" % i) for i in range(3)]
        S_vc = [P.new_sem("vc%d" % i) for i in range(3)]
        S_p = P.new_sem("p")
        S_vn = [P.new_sem("vn%d" % i) for i in range(5)]
        S_vr = [P.new_sem("vr%d" % i) for i in range(4)]
        S_wk = [P.new_sem("wk%d" % i) for i in range(7)]
        Bout = Buf("out_dram")
        Bvscr = [Buf("vscr%d" % i) for i in range(8)]

        mhalf = cols[:, 17:32]

        Bmh = Buf("mhalf")
        P.dma("sp", cols[:, 0:8], gn_col[:, :], S_c, writes=[Bcols], nbytes=4096, par=True)
        P.dma("sp", cols[:, 8:16], bcol_d[:, :], S_c, writes=[Bcols], nbytes=4096, par=True)
        P.dma("sp", cols[:, 16:17], gcol_d[:, :], S_c, writes=[Bcols], nbytes=512, par=True)
        P.dma("sp", cols[:, 32:40], gocol_d[:, :], S_c, writes=[Bcols], nbytes=4096, par=True)
        P.dma("sp", identf[:], ident_d[:, :], S_c, writes=[Bid], par=True)
        for m in range(5):
            P.dma("sp", mstage[:, m, :], masks_d[m, :, :], S_c, writes=Bstg[0], par=True)
        P.dma("sp", gbc[:, 0, :], gvec[0:1, :].partition_broadcast(128), S_c, writes=[Bgbc], nbytes=262144, par=True)
        P.dma("sp", gk64[:], gvec[3:4, 0:64].partition_broadcast(128), S_c, writes=[Bgbc], nbytes=32768, par=True)
        P.dma("sp", gple[:], gple_d[0:1, :].partition_broadcast(128), S_c, writes=[Bgple], nbytes=524288, par=True)
        for gI in range(4):
            P.dma("sp", wsf[:, gI, :], wsT_d[gI, :, :], S_c, writes=[BxinA], par=True)
            P.dma("sp", wsf[:, 4 + gI, :], wsS_d[gI, :, :], S_c, writes=[BxinA], par=True)
        cids = [o.idx for o in P.ops if o.dma_sem is S_c]
        for Bc in (Bcols, Bid, Bgbc, Bgple, BxinA, Bstg[0][0], Bstg[0][1]):
            Bc.w = list(cids)
        MEMSET("pool", cols[:, 17:32], -0.5, [Bmh])
        CP("dve", identb[:], identf[:], [Bid], [Bid])
        CP("dve", masks[:], mstage[:], Bstg[0], [Bmask])
        P.op("dve", lambda e: e.tensor_tensor(wsb[:], wsf[:], mstage[:, 0:1, :].to_broadcast([128, 8, 128]), ALU.mult),
             [BxinA] + Bstg[0], [Bws], 1.2)
        MEMSET("pool", vnat[:], 1.0, Bvnat)

        S_stg = [S_kc[0], S_kc[1], S_kc[2], S_vc[0]]

        def load_w(dst, src, ncols, kcs, scale_off, name):
            res = []
            for c0 in range(0, ncols, 1024):
                cw = min(1024, ncols - c0)
                bl = []
                for kc in range(kcs):
                    Bd = Buf("%s_%d_%d" % (name, c0, kc))
                    bl.append(Bd)
                    xi = nxt("stg", 4)
                    P.dma("sp", stg[xi][:, 0:cw], src[kc * 128:(kc + 1) * 128, c0:c0 + cw], S_stg[xi], writes=Bstg[xi],
                          nbytes=128 * cw * 4)
                    if scale_off is not None:
                        ACT(dst[:, kc, c0:c0 + cw], stg[xi][:, 0:cw], AF.Copy, Bstg[xi] + [Bcols], [Bd],
                            scale=cols[:, scale_off + kc:scale_off + kc + 1])
                    else:
                        CP("dve", dst[:, kc, c0:c0 + cw], stg[xi][:, 0:cw], Bstg[xi], [Bd])
                res.append(bl)
            return res

        Bw_late = {}

        def load_late_weights():
            Bw_late["wout"] = load_w(wout, w_out, 1024, 8, 32, "wout")[0]
            Bw_late["wgate"] = load_w(wgate, w_gate, 1024, 8, None, "wgate")[0]
            Bw_late["wple"] = load_w(wple, w_ple, 1024, 2, None, "wple")[0]

        def rstd_calc(stat, Bstat, G, width, c_ssq, c_tmp, c_r):
            TS("dve", stat[:, c_tmp:c_tmp + G], stat[:, c_ssq:c_ssq + G], 1.0 / width, EPS, ALU.mult, ALU.add, [Bstat], [Bstat])
            TT("pool", stat[:, c_r:c_r + G], stat[:, c_tmp:c_tmp + G], mhalf[:, 0:G], ALU.pow, [Bstat, Bmh], [Bstat])

        def group_ssq(src, Bsrc, G, stat, Bstat, c_ssq, scratch, Bscr, scale=None, extraR=()):
            if scale is None:
                ACT(scratch, src, AF.Square, [Bsrc], [Bscr])
            else:
                ACT(scratch, src, AF.Square, [Bsrc] + list(extraR), [Bscr], scale=scale)
            P.op("dve", lambda e: e.tensor_reduce(out=stat[:, c_ssq:c_ssq + G],
                                                  in_=scratch.rearrange("p (g w) -> p g w", g=G),
                                                  axis=AX.X, op=ALU.add), [Bscr], [Bstat], 0.65)

        def transposes8(src_bf, Bsrc, nchunks, dst3, Bdst, scale=None):
            pi = nxt("ptr", 2)
            TRS([(ptr[pi][:, c * 128:(c + 1) * 128], src_bf[:, c * 128:(c + 1) * 128], identb[:]) for c in range(nchunks)],
                [Bsrc, Bid], [Bptr[pi]])
            src3 = ptr[pi][:, 0:nchunks * 128].rearrange("p (c t) -> p c t", c=nchunks)
            if scale is None:
                CP("dve", dst3, src3, [Bptr[pi]], [Bdst])
            else:
                ACT(dst3, src3, AF.Copy, [Bptr[pi], Bcols], [Bdst], scale=scale)

        u, Bu = wk[:, 0, :], Bwk[0]
        va, Bva = wk[:, 1, :], Bwk[1]
        za, Bza = wk[:, 2, :], Bwk[2]
        q, Bq = wk[:, 3, :], Bwk[3]
        k, Bk = wk[:, 4, :], Bwk[4]
        v, Bv = wk[:, 5, :], Bwk[5]
        sq, Bsq = wk[:, 6, :], Bwk[6]

        def A0(x_src):
            si = nxt("stx", 2)
            stx = stx2[:, si, :]; Bstx = Bstx2[si]
            P.dma("sp", xinA[:], x_src, S_x[0], writes=[BxinA], nbytes=524288)
            x_t = xinA; Bx = BxinA
            MEMSET("dve", stx, 0.0, [Bstx])
            ACT(b1A[:], x_t[:], AF.Square, [Bx, Bstx], [Bb1A, Bstx], accum=stx[:, 0:1])
            rstd_calc(stx, Bstx, 1, 1024.0, 0, 1, 2)
            ACT(b1A[:], x_t[:], AF.Copy, [Bx], [Bb1A])
            transposes8(b1A, Bb1A, 8, tT[0][:], BtT[0])
            return dict(xT=tT[0], BxT=BtT[0], stx=stx, Bstx=Bstx)

        def A1(c, dst):
            xT, BxT = c["xT"], c["BxT"]
            stx, Bstx = c["stx"], c["Bstx"]
            rx = stx[:, 2:3]

            def proj(j):
                pi = nxt("pm", 4)
                MMS([(pm[pi][:], xT[:, kc, :], win[:, kc, j * 512:(j + 1) * 512], kc == 0, kc == 7) for kc in range(8)],
                    [BxT] + Bw_late["win"][(j * 512) // 1024], [Bpm[pi]])
                return pi

            pi = proj(1)
            ACT(va, pm[pi][:], AF.Copy, [Bpm[pi], Bstx], [Bva], scale=rx)
            group_ssq(pm[pi][:], Bpm[pi], 4, stv, Bstv, 0, sq, Bsq, scale=rx, extraR=[Bstx])
            pi = proj(0)
            ACT(u, pm[pi][:], AF.Copy, [Bpm[pi], Bstx], [Bu], scale=rx)
            pi = proj(2)
            ACT(za, pm[pi][:], AF.Silu, [Bpm[pi], Bstx], [Bza], scale=rx)
            pi = proj(4)
            ACT(k, pm[pi][:], AF.Copy, [Bpm[pi], Bstx], [Bk], scale=rx)
            group_ssq(pm[pi][:], Bpm[pi], 8, stk, Bstk, 0, sq, Bsq, scale=rx, extraR=[Bstx])
            pi = proj(3)
            ACT(q, pm[pi][:], AF.Copy, [Bpm[pi], Bstx], [Bq], scale=rx)
            group_ssq(pm[pi][:], Bpm[pi], 8, stq, Bstq, 0, sq, Bsq, scale=rx, extraR=[Bstx])
            pi = proj(5)
            ACT(v, pm[pi][:], AF.Copy, [Bpm[pi], Bstx], [Bv], scale=rx)
            ACT(dst["vbf"], pm[pi][:].rearrange("p (h d) -> p h d", h=8), AF.Copy, [Bpm[pi], Bstx], [dst["Bvbf"]], scale=rx)
            P.dma("pool", dst["vout"], v, S_wk[5], reads=[Bv], nbytes=262144)
            pi = proj(6)
            ACT(dst["zb"], pm[pi][:], AF.Silu, [Bpm[pi], Bstx], [dst["Bzb"]], scale=rx)

        def A2(wsoff, bcoff, dst, sample):
            rstd_calc(stv, Bstv, 4, 128.0, 0, 4, 8)
            va3 = va.rearrange("p (g w) -> p g w", g=4)
            TT("dve", va3, va3, stv[:, 8:12].unsqueeze(2).to_broadcast([128, 4, 128]), ALU.mult, [Bva, Bstv], [Bva])
            if sample:
                TT("dve", va, va, gbc[:, 0, :], ALU.mult, [Bva, Bgbc], [Bva])
                CP("dve", vnb[:], va, [Bva], [Bvnb])
                P.dma("pool", vac[:, :], va, S_wk[1], reads=[Bva], nbytes=262144)
            else:
                TT("dve", vnb[:], va, gbc[:, 0, :], ALU.mult, [Bva, Bgbc], [Bvnb])
            pg = nxt("pm", 4)
            MMS([(pm[pg][:, g * 128:(g + 1) * 128], wsb[:, wsoff + g, :], vnb[:, g * 128:(g + 1) * 128], True, True) for g in range(4)],
                [Bws, Bvnb], [Bpm[pg]])
            rstd_calc(stk, Bstk, 8, 64.0, 0, 8, 16)
            k3 = k.rearrange("p (g w) -> p g w", g=8)
            TT("dve", k3, k3, stk[:, 16:24].unsqueeze(2).to_broadcast([128, 8, 64]), ALU.mult, [Bk, Bstk], [Bk])
            TT("dve", k3, k3, gk64[:].unsqueeze(1).to_broadcast([128, 8, 64]), ALU.mult, [Bk, Bgbc], [Bk])
            P.dma("pool", dst["kout"], k, S_wk[4], reads=[Bk], nbytes=262144)
            for g in range(4):
                sl = slice(g * 128, (g + 1) * 128)
                P.op("dve", (lambda sl=sl, g=g: (lambda e: e.scalar_tensor_tensor(
                    out=u[:, sl], in0=pm[pg][:, sl], scalar=cols[:, 8 + bcoff + g:9 + bcoff + g], in1=u[:, sl],
                    op0=ALU.add, op1=ALU.mult)))(), [Bpm[pg], Bcols, Bu], [Bu], 0.3)
            group_ssq(u, Bu, 4, sta, Bsta, 0, sq, Bsq)
            CP("act", kb[:], k, [Bk], [Bkb])
            transposes8(kb, Bkb, 4, dst["kT"], dst["BkT"])
            rstd_calc(sta, Bsta, 4, 128.0, 0, 4, 8)
            rstd_calc(stq, Bstq, 8, 64.0, 0, 8, 16)
            TT("dve", qb[:].rearrange("p (g w) -> p g w", g=8), q.rearrange("p (g w) -> p g w", g=8),
               stq[:, 16:24].unsqueeze(2).to_broadcast([128, 8, 64]), ALU.mult, [Bq, Bstq], [Bqb])
            transposes8(qb, Bqb, 4, dst["qT"], dst["BqT"], scale=cols[:, 16:17])
            u3 = u.rearrange("p (g w) -> p g w", g=4)
            TT("dve", u3, u3, sta[:, 8:12].unsqueeze(2).to_broadcast([128, 4, 128]), ALU.mult, [Bu, Bsta], [Bu])
            TT("dve", dst["aout"], u, za, ALU.mult, [Bu, Bza], [dst["Baout"]])

        def phaseA(x_src, wsoff, bcoff, dst, sample):
            c = A0(x_src)
            A1(c, dst)
            A2(wsoff, bcoff, dst, sample)

        def phaseC(x_src, p_src, y_dst, tcol, aout_ap, Baout_b, zb_ap, Bzb_b, ncolsel):
            y, By = wc[:, 0, :], Bwc[0]
            hh, Bh = wc[:, 1:3, :].rearrange("p a b -> p (a b)"), [Bwc[1], Bwc[2]]
            ee, Be = wc[:, 3:5, :].rearrange("p a b -> p (a b)"), [Bwc[3], Bwc[4]]
            P.dma("sp", hh, x_src, S_x[1], writes=Bh, nbytes=524288)
            P.dma("sp", pin[:], p_src, S_p, writes=[Bpin], nbytes=131072)
            CP("act", pbf[:], pin[:], [Bpin], [Bpbf])
            pi_t = nxt("ptr", 2)
            TRS([(ptr[pi_t][:, c * 128:(c + 1) * 128], pbf[:, c * 128:(c + 1) * 128], identb[:]) for c in range(2)],
                [Bpbf, Bid], [Bptr[pi_t]])
            CP("dve", pTt[:], ptr[pi_t][:, 0:256].rearrange("p (c t) -> p c t", c=2), [Bptr[pi_t]], [BpTt])
            MEMSET("dve", ste[:], 0.0, [Bste])
            for j in range(2):
                pi = nxt("pm", 4)
                MMS([(pm[pi][:], pTt[:, kc, :], wple[:, kc, j * 512:(j + 1) * 512], kc == 0, kc == 1) for kc in range(2)],
                    [BpTt] + Bw_late["wple"], [Bpm[pi]])
                ACT(ee[:, j * 512:(j + 1) * 512], pm[pi][:], AF.Square, [Bpm[pi], Bste], [Be[j], Bste], accum=ste[:, j:j + 1])
                CP("act", ee[:, j * 512:(j + 1) * 512], pm[pi][:], [Bpm[pi]], [Be[j]])
            TT("dve", ste[:, 2:3], ste[:, 0:1], ste[:, 1:2], ALU.add, [Bste], [Bste])
            rstd_calc(ste, Bste, 1, 1024.0, 2, 3, 4)
            for j in range(2):
                sl = slice(j * 512, (j + 1) * 512)
                P.op("dve", (lambda sl=sl: (lambda e: e.scalar_tensor_tensor(
                    out=ee[:, sl], in0=ee[:, sl], scalar=ste[:, 4:5], in1=gple[:, sl], op0=ALU.mult, op1=ALU.mult)))(),
                    [Be[j], Bste, Bgple], [Be[j]], 0.65)
            for half in range(2):
                TRS([(pa[half][:, hl * 65:(hl + 1) * 65], oacc[0:65, half * 4 + hl, tcol], identf[0:65, 0:65]) for hl in range(4)],
                    [Boacc[half * 4 + hl] for hl in range(4)] + [Bid], [Bpa[half]])
            for half in range(2):
                pa3 = pa[half][:, 0:260].rearrange("p (h f) -> p h f", h=4)
                P.op("dve", (lambda pa3=pa3, half=half: (lambda e: e.reciprocal(stb[:, half * 4:half * 4 + 4].unsqueeze(2), pa3[:, :, 64:65])))(),
                     [Bpa[half]], [Bstb], 0.15)
                TT("dve", y[:, half * 256:(half + 1) * 256].rearrange("p (h d) -> p h d", h=4), pa3[:, :, 0:64],
                   stb[:, half * 4:half * 4 + 4].unsqueeze(2).to_broadcast([128, 4, 64]), ALU.mult, [Bpa[half], Bstb], [By])
            group_ssq(y, By, 8, stb, Bstb, 8, pa[0][:], Bpa[0])
            rstd_calc(stb, Bstb, 8, 64.0, 8, 16, 24)
            y3 = y.rearrange("p (g w) -> p g w", g=8)
            TT("dve", y3, y3, stb[:, 24:32].unsqueeze(2).to_broadcast([128, 8, 64]), ALU.mult, [By, Bstb], [By])
            TT("dve", b1[:, 512:1024], y, zb_ap, ALU.mult, [By, Bzb_b], [Bb1])
            CP("act", b1[:, 0:512], aout_ap, [Baout_b], [Bb1])
            transposes8(b1, Bb1, 8, tT[1][:], BtT[1])
            mT = tT[1]; BmT = BtT[1]
            for j in range(2):
                pi = nxt("pm", 4)
                MMS([(pm[pi][:], mT[:, kc, :], wout[:, kc, j * 512:(j + 1) * 512], kc == 0, kc == 7) for kc in range(8)],
                    [BmT] + Bw_late["wout"], [Bpm[pi]])
                TT("dve", hh[:, j * 512:(j + 1) * 512], pm[pi][:], hh[:, j * 512:(j + 1) * 512], ALU.add, [Bpm[pi], Bh[j]], [Bh[j]])
            CP("act", b1[:], hh, Bh, [Bb1])
            transposes8(b1, Bb1, 8, tT[1][:], BtT[1])
            hT = tT[1]; BhT = BtT[1]
            pg_ = []
            for j in range(2):
                pi = nxt("pm", 4)
                pg_.append(pi)
                MMS([(pm[pi][:], hT[:, kc, :], wgate[:, kc, j * 512:(j + 1) * 512], kc == 0, kc == 7) for kc in range(8)],
                    [BhT] + Bw_late["wgate"], [Bpm[pi]])
                ACT(pm[pi][:], pm[pi][:], AF.Sigmoid, [Bpm[pi]], [Bpm[pi]])
            for j in range(2):
                sl = slice(j * 512, (j + 1) * 512)
                TT("dve", ee[:, sl], pm[pg_[j]][:], ee[:, sl], ALU.mult, [Be[j], Bpm[pg_[j]]], [Be[j]])
                TT("dve", ee[:, sl], hh[:, sl], ee[:, sl], ALU.add, [Bh[j], Be[j]], [Be[j]])
            P.dma("pool", y_dst, ee, S_wk[1], reads=Be, nbytes=524288)

        def mk_dst(g, tt):
            s, jb = divmod(g, 4)
            i = 4 * jb + tt
            row = s * 2048 + i * 128
            slot = i % 5
            return row, slot, dict(
                aout=aout[:, tt, :], Baout=Baout[tt],
                qT=qT[:, :, tt * 128:(tt + 1) * 128], BqT=BqT,
                kT=kT[:, :, i * 128:(i + 1) * 128], BkT=BkT[jb],
                kout=kw[row:row + 128, :], vout=vw[row:row + 128, :],
                vbf=vnat[:, slot, :].rearrange("p (h f) -> p h f", h=8)[:, :, 0:64], Bvbf=Bvnat[slot],
                zb=zbg[:, tt, :], Bzb=Bzbg[tt],
            )

        def A_tile(g, tt, ctx):
            s, jb = divmod(g, 4)
            row, slot, dst = mk_dst(g, tt)
            A1(ctx, dst)
            P.dma("pool", vscr[row:row + 128, :], vnat[:, slot, :], S_vn[slot], reads=[Bvnat[slot]], writes=[Bvscr[g]], nbytes=133120, par=True)
            nctx = None
            nrow = row + 128
            if nrow < NSEQ * 2048 and (tt < 3 or g + 1 < NG):
                nctx = A0(xp[nrow:nrow + 128, :])
            A2(0, 0, dst, False)
            if tt == 3:
                base = s * 2048 + jb * 512
                for c4 in range(4):
                    srcv = vscr[base:base + 512, :].rearrange("(m c) f -> c m f", c=4)[c4]
                    P.dma("sp", vr4[:, c4 * 4 + jb, :], srcv, S_vr[jb], reads=[Bvscr[g]], writes=[Bvr4[jb]], nbytes=133120, par=True)
            return nctx

        def C_tile(g, tt):
            s, jb = divmod(g, 4)
            row = s * 2048 + (4 * jb + tt) * 128
            phaseC(xp[row:row + 128, :], pp[row:row + 128, :], yp[row:row + 128, :],
                   slice(tt * 128, (tt + 1) * 128), aout[:, tt, :], Baout[tt], zbg[:, tt, :], Bzbg[tt], None)

        def B_group(g):
            s, jb = divmod(g, 4)
            P.mark("B g%d" % g)
            units = []
            for h in range(8):
                hp, r0 = h // 2, 64 * (h % 2)
                kTh = kT[r0:r0 + 64, hp, :]
                qTh = qT[r0:r0 + 64, hp, :]
                qcls = qTh.rearrange("p (m c) -> p c m", c=4)
                tts = [tt for tt in range(4) if 4 * jb + tt >= 1]

                def nat_unit(h=h, kTh=kTh, qTh=qTh, tts=tts):
                    st_ = {}

                    def qk():
                        pd = nxt("pm", 4)
                        MMS([(pm[pd][:, tt * 128:(tt + 1) * 128], kTh[:, (4 * jb + tt) * 128:(4 * jb + tt + 1) * 128],
                              qTh[:, tt * 128:(tt + 1) * 128], True, True) for tt in range(4)],
                            [BkT[jb], BqT], [Bpm[pd]])
                        pu = nxt("pm", 4)
                        MMS([(pm[pu][:, tt * 128:(tt + 1) * 128], kTh[:, (4 * jb + tt - 1) * 128:(4 * jb + tt) * 128],
                              qTh[:, tt * 128:(tt + 1) * 128], True, True) for tt in tts],
                            [BkT[jb], BkT[max(jb - 1, 0)], BqT], [Bpm[pu]])
                        st_["pd"], st_["pu"] = pd, pu

                    def em():
                        pd, pu = st_["pd"], st_["pu"]
                        pti = nxt("pT", NPT)
                        pTn = pT[pti]; BpTn = BpT[pti]
                        st_["pT"] = (pTn, BpTn)
                        ACT(pTn[:, 0:512], pm[pd][:], AF.Exp, [Bpm[pd]], [BpTn], scale=SCALE)
                        lo = tts[0] * 128
                        ACT(pTn[:, 512 + lo:1024], pm[pu][:, lo:512], AF.Exp, [Bpm[pu]], [BpTn], scale=SCALE)
                        TT("dve", pTn[:, 0:512].rearrange("p (a b) -> p a b", a=4), pTn[:, 0:512].rearrange("p (a b) -> p a b", a=4),
                           masks[:, 0:1, :].to_broadcast([128, 4, 128]), ALU.mult, [BpTn, Bmask], [BpTn])
                        nu = len(tts)
                        TT("dve", pTn[:, 512 + lo:1024].rearrange("p (a b) -> p a b", a=nu), pTn[:, 512 + lo:1024].rearrange("p (a b) -> p a b", a=nu),
                           masks[:, 1:2, :].to_broadcast([128, nu, 128]), ALU.mult, [BpTn, Bmask], [BpTn])

                    def pv():
                        pTn, BpTn = st_["pT"]
                        items = []
                        for tt in range(4):
                            i = 4 * jb + tt
                            has_prev = i >= 1
                            items.append((pa[0][0:65, tt * 128:(tt + 1) * 128], vnat[:, i % 5, h * 65:(h + 1) * 65],
                                          pTn[:, tt * 128:(tt + 1) * 128], True, not has_prev))
                            if has_prev:
                                items.append((pa[0][0:65, tt * 128:(tt + 1) * 128], vnat[:, (i - 1) % 5, h * 65:(h + 1) * 65],
                                              pTn[:, 512 + tt * 128:512 + (tt + 1) * 128], False, True))
                        MMS(items, [BpTn] + Bvnat, [Bpa[0]])
                    return qk, em, pv

                units.append(nat_unit())

                for jb2 in range(jb + 1):
                    def r4_unit(h=h, kTh=kTh, qcls=qcls, jb2=jb2):
                        st_ = {}

                        def qk():
                            pr = nxt("pm", 4)
                            kcls = kTh[:, jb2 * 512:(jb2 + 1) * 512].rearrange("p (m c) -> p c m", c=4)
                            MMS([(pm[pr][:, c4 * 128:(c4 + 1) * 128], kcls[:, c4, :], qcls[:, c4, :], True, True) for c4 in range(4)],
                                [BkT[jb2], BqT], [Bpm[pr]])
                            st_["pr"] = pr

                        def em():
                            pr = st_["pr"]
                            pti2 = nxt("pT", NPT)
                            pTr = pT[pti2]; BpTr = BpT[pti2]
                            st_["pT"] = (pTr, BpTr)
                            ACT(pTr[:, 0:512], pm[pr][:], AF.Exp, [Bpm[pr]], [BpTr], scale=SCALE)
                            mi = 2 + min(jb - jb2, 2)
                            TT("dve", pTr[:, 0:512].rearrange("p (a b) -> p a b", a=4), pTr[:, 0:512].rearrange("p (a b) -> p a b", a=4),
                               masks[:, mi:mi + 1, :].to_broadcast([128, 4, 128]), ALU.mult, [BpTr, Bmask], [BpTr])

                        def pv():
                            pTr, BpTr = st_["pT"]
                            MMS([(pa[1][0:65, c4 * 128:(c4 + 1) * 128], vr4[:, c4 * 4 + jb2, h * 65:(h + 1) * 65],
                                  pTr[:, c4 * 128:(c4 + 1) * 128], (jb2 == 0 and c4 == 0), jb2 == jb) for c4 in range(4)],
                                [BpTr, Bvr4[jb2]], [Bpa[1]])
                            if jb2 == jb:
                                CP("dve", oacc[0:65, h, :], pa[0][0:65, :], [Bpa[0]], [Boacc[h]])
                                TT("dve", oacc[0:65, h, :].rearrange("p (m c) -> p c m", c=4),
                                   pa[1][0:65, :].rearrange("p (c m) -> p c m", c=4),
                                   oacc[0:65, h, :].rearrange("p (m c) -> p c m", c=4), ALU.add, [Bpa[1], Boacc[h]], [Boacc[h]])
                        return qk, em, pv
                    units.append(r4_unit())

            units[0][0]()
            for n in range(len(units)):
                if n + 1 < len(units):
                    units[n + 1][0]()
                units[n][1]()
                units[n][2]()


        NG = NSEQ * NJB
        assert NJB == 4
        P.mark("A g0")
        ctx = A0(xp[0:128, :])
        Bw_late["win"] = load_w(win, w_in, 3584, 8, 0, "win")
        for tt in range(4):
            ctx = A_tile(0, tt, ctx)
        load_late_weights()
        B_group(0)
        for g in range(NG - 1):
            P.mark("CA g%d" % g)
            for tt in range(4):
                C_tile(g, tt)
                ctx = A_tile(g + 1, tt, ctx)
            B_group(g + 1)
        P.mark("sample")
        P.barrier()
        for tt in range(4):
            C_tile(NG - 1, tt)

        Bs = {n: Buf("s_" + n) for n in ("kTn", "qbd", "mc", "mn", "pN")}
        kTn = big[:, 0:512].rearrange("p (c t) -> p c t", c=4)
        qbd = big[:, 512:1536].rearrange("p (c b j) -> p c b j", c=4, b=16)
        mcb = big[:, 1536:2368]
        mnb = big[:, 2368:3392]
        NKB, NVB = 8, 11
        kcb = [big[:, 3392 + i * 512:3392 + (i + 1) * 512] for i in range(4)] + [big[:, 15832:16344], aout[:, 1, :], aout[:, 2, :], aout[:, 3, :]]
        Bkc = [[Buf("kc%d" % i)] for i in range(5)] + [[Baout[1]], [Baout[2]], [Baout[3]]]
        S_kb = S_vn[0:5] + [S_wk[2], S_wk[3], S_wk[6]]
        kcT = [big[:, 6464 + i * 512:6464 + (i + 1) * 512].rearrange("p (c t) -> p c t", c=4) for i in range(2)]
        BkcT = [Buf("kcT%d" % i) for i in range(2)]
        vc = [big[:, 7488 + i * 1024:7488 + (i + 1) * 1024].bitcast(F32) for i in range(3)] + [big[:, 5440:6464].bitcast(F32), big[:, 14808:15832].bitcast(F32)]
        vc += [vnat[:, 1 + 2 * i:3 + 2 * i, :].rearrange("p a b -> p (a b)")[:, 0:1024].bitcast(F32) for i in range(2)]
        vc += [wk[:, i, :] for i in range(4)]
        Bvc = [[Buf("vc%d" % i)] for i in range(5)] + [[Bvnat[1], Bvnat[2]], [Bvnat[3], Bvnat[4]]] + [[Bwk[i]] for i in range(4)]
        S_vb = [S_vc[0], S_vc[1], S_vc[2], S_kc[0], S_kc[1], S_kc[2], S_wk[0]] + S_vr[0:4]
        vaug = [big[:, 10560 + i * 520:10560 + (i + 1) * 520] for i in range(3)]; Bvaug = [Buf("vaug%d" % i) for i in range(3)]
        pS = [big[:, 12120 + i * 832:12120 + (i + 1) * 832] for i in range(2)]; BpS = [Buf("pS%d" % i) for i in range(2)]
        pN = big[:, 13784:14808]
        ptrf = [ptr[i][:].bitcast(F32) for i in range(2)]
        S_m = P.new_sem("m2")
        P.dma("pool", mcb[:, 0:640], maskc_d[:, :], S_m, writes=[Bs["mc"]], nbytes=327680)
        P.dma("pool", mnb, maskn_d[:, :], S_m, writes=[Bs["mn"]], nbytes=524288)
        mids = [o.idx for o in P.ops if o.dma_sem is S_m]
        Bs["mc"].w = list(mids)
        Bs["mn"].w = list(mids)
        MEMSET("dve", big[:, 512:1536], 0.0, [Bs["qbd"]])
        MEMSET("pool", big[:, 10560:10560 + 3 * 520], 1.0, Bvaug)
        MEMSET("pool", vnat[:, 0, :], 1.0, [Bvnat[0]])

        dstS = dict(aout=aout[:, 0, :], Baout=Baout[0], qT=qT[:, :, 0:128], BqT=BqT, kT=kTn, BkT=Bs["kTn"],
                    kout=kn[:, :], vout=vn_o[:, :],
                    vbf=vnat[:, 0, :].rearrange("p (h f) -> p h f", h=8)[:, :, 0:64], Bvbf=Bvnat[0],
                    zb=zbg[:, 0, :], Bzb=Bzbg[0])
        phaseA(xs[:, :], 4, 4, dstS, True)
        qs4 = qT[:, :, 0:128].rearrange("p c (b t) -> p c b t", t=8)
        CP("dve", qbd[0:64, :, :, 0:8], qs4[0:64], [BqT], [Bs["qbd"]])
        CP("dve", qbd[64:128, :, :, 8:16], qs4[64:128], [BqT], [Bs["qbd"]])
        pn = [nxt("pm", 4), nxt("pm", 4)]
        for half in range(2):
            MMS([(pm[pn[half]][:, (hp % 2) * 256:(hp % 2 + 1) * 256], kTn[:, hp, :],
                  qbd[:, hp, :, :].rearrange("p b j -> p (b j)"), True, True) for hp in (2 * half, 2 * half + 1)],
                [Bs["kTn"], Bs["qbd"]], [Bpm[pn[half]]])
            ACT(pN[:, half * 512:(half + 1) * 512], pm[pn[half]][:], AF.Exp, [Bpm[pn[half]]], [Bs["pN"]], scale=SCALE)
        TT("dve", pN, pN, mnb, ALU.mult, [Bs["pN"], Bs["mn"]], [Bs["pN"]])
        pN5 = pN.rearrange("p (c b g t) -> p c b g t", c=4, b=16, g=2)
        for half in range(2):
            MMS([(pa[half][0:65, hl * 128:(hl + 1) * 128], vnat[:, 0, (half * 4 + hl) * 65:(half * 4 + hl + 1) * 65],
                  pN5[:, (half * 4 + hl) // 2, :, (half * 4 + hl) % 2, :], True, True) for hl in range(4)],
                [Bs["pN"], Bvnat[0]], [Bpa[half]])
            CP("act", oacc[0:65, half * 4:half * 4 + 4, 0:128], pa[half][0:65, :].rearrange("p (h t) -> p h t", h=4),
               [Bpa[half]], [Boacc[half * 4 + hl] for hl in range(4)])
        NT = 10

        def tile_srcs(tl):
            if tl < 8:
                return [(slice(0, 128), slice(tl, tl + 16 * 127 + 1, 16))]
            res = []
            for r in range(4):
                st0 = WB - 512 + 8 + 4 * (tl - 8) + r
                res.append((slice(32 * r, 32 * r + 32), slice(st0, st0 + 16 * 31 + 1, 16)))
            return res

        def cache_load(queue, dst, Bdst, sem, src3, b, tl):
            srcs = tile_srcs(tl)
            for (psl, rsl) in srcs:
                P.dma(queue, dst[psl, :], src3[b, rsl, :], sem, writes=Bdst, nbytes=2048 * (psl.stop - psl.start),
                      par=(len(srcs) > 1))

        cs = {"kc": 0, "kcT": 0, "vc": 0, "vaug": 0}
        sc_banks = {}

        def kside(b):
            pi = nxt("pS", 2)
            banks = (nxt("pm", 4), nxt("pm", 4))
            for tl in range(NT):
                ki = cs["kc"] % NKB; cs["kc"] += 1
                cache_load("pool", kcb[ki], Bkc[ki], S_kb[ki], ck, b, tl)
                pti = nxt("ptr", 2)
                TRS([(ptr[pti][:, c * 128:(c + 1) * 128], kcb[ki][:, c * 128:(c + 1) * 128], identb[:]) for c in range(4)],
                    Bkc[ki] + [Bid], [Bptr[pti]])
                kti = cs["kcT"] % 2; cs["kcT"] += 1
                CP("dve" if tl % 2 == 0 else "act", kcT[kti], ptr[pti][:, 0:512].rearrange("p (c t) -> p c t", c=4), [Bptr[pti]], [BkcT[kti]])
                bk = banks[0] if tl < 8 else banks[1]
                off = (tl % 8) * 64
                MMS([(pm[bk][:, off + hp * 16:off + (hp + 1) * 16], kcT[kti][:, hp, :], qbd[:, hp, b, :], True, True) for hp in range(4)],
                    [BkcT[kti], Bs["qbd"]], [Bpm[bk]])
            ACT(pS[pi][:, 0:512], pm[banks[0]][:], AF.Exp, [Bpm[banks[0]]], [BpS[pi]], scale=SCALE)
            ACT(pS[pi][:, 512:640], pm[banks[1]][:, 0:128], AF.Exp, [Bpm[banks[1]]], [BpS[pi]], scale=SCALE)
            TT("dve", pS[pi][:, 0:640], pS[pi][:, 0:640], mcb[:, 0:640], ALU.mult, [BpS[pi], Bs["mc"]], [BpS[pi]])
            return pi

        def vside(b, pi):
            for tl in range(NT):
                vi = cs["vc"] % NVB; cs["vc"] += 1
                cache_load("sp", vc[vi], Bvc[vi], S_vb[vi], cv, b, tl)
                ai = cs["vaug"] % 3; cs["vaug"] += 1
                CP(("act", "dve")[tl % 2], vaug[ai].rearrange("p (h f) -> p h f", h=8)[:, :, 0:64], vc[vi].rearrange("p (h d) -> p h d", h=8),
                   Bvc[vi], [Bvaug[ai]])
                MMS([(pa[h // 4][0:65, (h % 4) * 128 + 8 * b:(h % 4) * 128 + 8 * b + 8], vaug[ai][:, h * 65:(h + 1) * 65],
                      pS[pi][:, tl * 64 + h * 8:tl * 64 + h * 8 + 8], (b == 0 and tl == 0 and h % 4 == 0), tl == NT - 1) for h in range(8)],
                    [Bvaug[ai], BpS[pi]], [Bpa[0], Bpa[1]])

        nb = NB_S
        pis = {0: kside(0)}
        for b in range(nb):
            if b + 1 < nb:
                pis[b + 1] = kside(b + 1)
            vside(b, pis[b])
        for half in range(2):
            o3 = oacc[0:65, half * 4:half * 4 + 4, 0:128]
            TT("dve", o3, pa[half][0:65, :].rearrange("p (h t) -> p h t", h=4), o3, ALU.add,
               [Bpa[half]] + [Boacc[half * 4 + hl] for hl in range(4)], [Boacc[half * 4 + hl] for hl in range(4)])
        phaseC(xs[:, :], ps_[:, :], ys[:, :], slice(0, 128), aout[:, 0, :], Baout[0], zbg[:, 0, :], Bzbg[0], None)

        P.barrier()
        P.emit()
    return nc


def _consts():
    ki = np.arange(128)[:, None]
    qi = np.arange(128)[None, :]
    D = (ki <= qi).astype(np.float32)
    U = (ki >= qi).astype(np.float32)
    m4 = ((qi - ki) % 4 == 0).astype(np.float32)
    R0 = D * (1.0 + m4)
    R1 = U + m4
    R2 = m4
    masks = np.stack([D, U, R0, R1, R2]).astype(np.float32)
    mc = np.zeros((128, 10, 8, 8), np.float32)
    p = np.arange(128)
    for tl in range(10):
        if tl < 8:
            row = tl + 16 * p
        else:
            row = WB - 512 + 8 + 4 * (tl - 8) + (p // 32) + 16 * (p % 32)
        for t in range(8):
            mult = ((row % 16 == t % 16).astype(np.float32)
                    + ((row % 4 == t % 4) & (row >= WB - 512 + t)).astype(np.float32)
                    + (row >= WB - 128 + t).astype(np.float32))
            mc[:, tl, :, t] = mult[:, None]
    mn = np.zeros((16, 8, 4, 16, 2, 8), np.float32)
    for b in range(16):
        for t2 in range(8):
            for t in range(8):
                d = t - t2
                if d >= 0:
                    mn[b, t2, :, b, :, t] = 1.0 + (1.0 if d % 4 == 0 else 0.0) + (1.0 if d == 0 else 0.0)
    return masks, mc.reshape(128, 10 * 64), mn.reshape(128, 1024)


_NC_CACHE = {}


def kernel(x_prompt, x_sample, cache_k, cache_v, p_prompt, p_sample, g_norm, w_in, w_s, b_s,
           g_va, g_oa, g_q, g_k, g_ob, w_out, w_ple, g_ple, w_ple_gate):
    f = lambda a: np.ascontiguousarray(np.asarray(a, dtype=np.float32))
    x_prompt = f(x_prompt); x_sample = f(x_sample); cache_k = f(cache_k); cache_v = f(cache_v)
    p_prompt = f(p_prompt); p_sample = f(p_sample)
    masks, maskc, maskn = _consts()
    ws = f(w_s)[0]
    wsT = np.ascontiguousarray(ws.transpose(0, 2, 1))
    wsS = np.zeros((4, 128, 128), np.float32)
    for b in range(16):
        wsS[:, 8 * b:8 * b + 8, 8 * b:8 * b + 8] = wsT[:, 0:8, 0:8]
    bs = f(b_s)[0]
    bcol = np.zeros((128, 8), np.float32)
    bcol[:, 0:4] = bs.T
    bcol[:, 4:8] = np.tile(bs[:, 0:8].T, (16, 1))
    gvec = np.stack([f(g_va)[0].reshape(512), f(g_oa)[0].reshape(512), f(g_ob)[0].reshape(512),
                     np.tile(f(g_k)[0], 8)]).astype(np.float32)
    shared = dict(
        w_in=f(w_in)[0], w_out=f(w_out)[0], w_gate=f(w_ple_gate)[0], w_ple=f(w_ple)[0],
        gn_col=np.ascontiguousarray(f(g_norm)[0].reshape(8, 128).T),
        wsT=wsT, wsS=wsS, bcol=bcol, gvec=np.ascontiguousarray(gvec),
        gple=f(g_ple)[0].reshape(1, 1024), gcol=np.tile(f(g_q)[0], 2).reshape(128, 1).astype(np.float32),
        gocol=np.ascontiguousarray(np.concatenate([f(g_oa)[0].reshape(512), f(g_ob)[0].reshape(512)]).reshape(8, 128).T),
        ident=np.eye(128, dtype=np.float32), masks=masks, maskc=maskc, maskn=maskn,
    )
    in_maps = []
    for c in range(NCORES):
        d = dict(shared)
        d["xp"] = x_prompt[2 * c:2 * c + 2].reshape(4096, 1024)
        d["pp"] = p_prompt[0, 2 * c:2 * c + 2].reshape(4096, 256)
        d["xs"] = x_sample[16 * c:16 * c + 16].reshape(128, 1024)
        d["ps"] = p_sample[0, 16 * c:16 * c + 16].reshape(128, 256)
        d["ck"] = cache_k[0, 16 * c:16 * c + 16].reshape(16, 2048, 512)
        d["cv"] = cache_v[0, 16 * c:16 * c + 16].reshape(16, 2048, 512)
        in_maps.append(d)
    if "nc" not in _NC_CACHE:
        _NC_CACHE["nc"] = build_nc()
    nc = _NC_CACHE["nc"]
    res = run_bass_kernel_spmd(nc, in_maps, core_ids=list(range(NCORES)))
    R = res.results
    y_p = np.concatenate([R[c]["yp"].reshape(2, 2048, 1024) for c in range(NCORES)], 0)
    y_s = np.concatenate([R[c]["ys"].reshape(16, 8, 1024) for c in range(NCORES)], 0)
    kwp = np.concatenate([R[c]["kw"].reshape(2, 2048, 8, 64) for c in range(NCORES)], 0)[None]
    vwp = np.concatenate([R[c]["vw"].reshape(2, 2048, 8, 64) for c in range(NCORES)], 0)[None]
    kns = np.concatenate([R[c]["kn"].reshape(16, 8, 8, 64) for c in range(NCORES)], 0)[None]
    vns = np.concatenate([R[c]["vn"].reshape(16, 8, 8, 64) for c in range(NCORES)], 0)[None]
    vas = np.concatenate([R[c]["vac"].reshape(16, 8, 512) for c in range(NCORES)], 0)[None]
    return (y_p.astype(np.float32), y_s.astype(np.float32), kwp.astype(np.float32), vwp.astype(np.float32),
            kns.astype(np.float32), vns.astype(np.float32), vas.astype(np.float32))
```

```python
import sys
import numpy as np
from contextlib import ExitStack
import concourse.bass as bass
import concourse.mybir as mybir
from concourse.bass_utils import run_bass_kernel_spmd

F32 = mybir.dt.float32
BF16 = mybir.dt.bfloat16
AF = mybir.ActivationFunctionType
ALU = mybir.AluOpType
AX = mybir.AxisListType

EPS = 1e-6
SCALE = 64 ** -0.5
NCORES = 8
NSEQ = 2
NJB = 4
NB_S = 16
WB = 2048


class Sem:
    def __init__(self, h, step):
        self.h = h
        self.step = step
        self.count = 0
        self.queue = None


class Buf:
    def __init__(self, name):
        self.name = name
        self.w = []
        self.r = []


class OpRec:
    __slots__ = ("idx", "eng", "fn", "deps", "dur", "dma_sem", "nbytes", "ev", "seg", "t0", "t1", "line")


SEM_LAT = 0.25
DMA_LAT = 1.8
DMA_BW = 230e3
DMA_ISSUE = {"sp": 0.35, "pool": 0.65}
WINDOW = 40
PRIO = 1
SLACK = 0.3
ENGS = ("pe", "act", "dve", "pool", "sp")


class Prog:
    def __init__(self, nc, stack):
        self.nc = nc
        self.stack = stack
        self.nsem = 0
        self.dsems = []
        self.esem = {}
        for n in ENGS:
            self.esem[n] = self.new_sem(n, 1)
        self.ops = []
        self.seg = 0
        self.seg_start = [0]
        self.marks = []

    def new_sem(self, name, step=16):
        self.nsem += 1
        h = self.stack.enter_context(self.nc.semaphore("s_%s_%d" % (name, self.nsem)))
        s = Sem(h, step)
        if step == 16:
            self.dsems.append(s)
        return s

    def sb(self, name, shape, dtype):
        return self.stack.enter_context(self.nc.sbuf_tensor("sb_" + name, list(shape), dtype))

    def ps(self, name, shape, dtype):
        return self.stack.enter_context(self.nc.psum_tensor("ps_" + name, list(shape), dtype))

    def _record(self, engname, fn, reads, writes, dur, dma_sem=None, nbytes=0, par=False):
        o = OpRec()
        o.idx = len(self.ops)
        o.eng = engname
        o.fn = fn
        o.dur = dur
        o.dma_sem = dma_sem
        o.nbytes = nbytes
        o.seg = self.seg
        o.ev = None
        f = sys._getframe(2)
        ln = []
        while f is not None and len(ln) < 4:
            if f.f_code.co_name not in ("ACT", "TT", "TS", "CP", "MEMSET", "MMS", "TRS", "op", "dma"):
                ln.append(f.f_lineno)
            f = f.f_back
        o.line = ln
        lo = self.seg_start[-1]
        deps = {}
        for b in reads:
            for w in b.w:
                if w >= lo:
                    deps[w] = True
        for b in writes:
            if not par:
                for w in b.w:
                    if w >= lo:
                        deps.setdefault(w, False)
            for r in b.r:
                if r >= lo:
                    deps.setdefault(r, False)
        o.deps = deps
        for b in reads:
            b.r.append(o.idx)
        for b in writes:
            if par:
                b.w.append(o.idx)
            else:
                b.w = [o.idx]
                b.r = []
        self.ops.append(o)
        return o

    def op(self, engname, fn, reads=(), writes=(), dur=0.5):
        return self._record(engname, fn, reads, writes, dur)

    def dma(self, engname, out, in_, sem, reads=(), writes=(), nbytes=65536, par=False):
        assert sem.queue in (None, engname)
        sem.queue = engname
        return self._record(engname, lambda e: e.dma_start(out=out, in_=in_), reads, writes,
                            DMA_ISSUE[engname], dma_sem=sem, nbytes=nbytes, par=par)

    def mark(self, name):
        self.marks.append((name, len(self.ops)))

    def barrier(self):
        self.seg += 1
        self.seg_start.append(len(self.ops))

    def _schedule(self, ops, t0):
        pending = {e: [] for e in ENGS}
        for o in ops:
            pending[o.eng].append(o)
        tail = {}
        succ = {}
        for o in ops:
            for d in o.deps:
                succ.setdefault(d, []).append(o)
        for o in reversed(ops):
            t = 0.0
            for s_ in succ.get(o.idx, ()):
                if tail[s_.idx] > t:
                    t = tail[s_.idx]
            tail[o.idx] = t + o.dur + (DMA_LAT + o.nbytes / DMA_BW if o.dma_sem is not None else 0.0)
        done = {}
        order = {e: [] for e in ENGS}
        eng_time = {e: t0 for e in ENGS}
        dma_free = t0
        remaining = len(ops)
        allops = self.ops
        tmax = t0
        while remaining:
            best = None
            for e in ENGS:
                et = eng_time[e]
                cnt = 0
                for o in pending[e]:
                    cnt += 1
                    if cnt > WINDOW:
                        break
                    ready = et
                    ok = True
                    for d in o.deps:
                        td = done.get(d)
                        if td is None:
                            ok = False
                            break
                        if allops[d].eng != e or allops[d].dma_sem is not None:
                            td += SEM_LAT
                        if td > ready:
                            ready = td
                    if not ok:
                        continue
                    if PRIO == 0:
                        key = (ready, o.idx)
                    else:
                        key = (max(ready, et + SLACK) if ready <= et + SLACK else ready, -tail[o.idx])
                    if best is None or key < best[0]:
                        best = (key, e, o, ready)
                    if PRIO == 0 and ready <= et:
                        break
            _, e, o, start = best
            pending[e].remove(o)
            order[e].append(o)
            if o.dma_sem is not None:
                iend = start + o.dur
                ts = max(iend, dma_free)
                dma_free = ts + o.nbytes / DMA_BW
                fin = dma_free + DMA_LAT
                eng_time[e] = iend
            else:
                fin = start + o.dur
                eng_time[e] = fin
            done[o.idx] = fin
            o.t0 = start
            o.t1 = fin
            if fin > tmax:
                tmax = fin
            remaining -= 1
        return order, tmax

    def emit(self):
        nc = self.nc
        nseg = self.seg + 1
        bounds = self.seg_start + [len(self.ops)]
        queues = {e: [] for e in ENGS}
        t = 0.0
        for s in range(nseg):
            ops = self.ops[bounds[s]:bounds[s + 1]]
            order, t = self._schedule(ops, t)
            for e in ENGS:
                for o in order[e]:
                    queues[e].append(o)
                queues[e].append(None)
        self.sim_time = t
        pos = {e: 0 for e in ENGS}
        for e in ENGS:
            for o in queues[e]:
                if o is None:
                    continue
                if o.dma_sem is not None:
                    o.dma_sem.count += 16
                    o.ev = (o.dma_sem, o.dma_sem.count)
                else:
                    pos[e] += 1
                    o.ev = (self.esem[e], pos[e])
        bar_targets = []
        cur = {}
        segops = [[] for _ in range(nseg)]
        for o in self.ops:
            segops[o.seg].append(o)
        for s in range(nseg):
            for o in segops[s]:
                S, v = o.ev
                if cur.get(S, 0) < v:
                    cur[S] = v
            bar_targets.append(dict(cur))
        prog = {}
        for e in ENGS:
            seen = {}
            out = []
            s = 0
            for o in queues[e]:
                if o is None:
                    waits = []
                    for S, v in bar_targets[s].items():
                        if S is self.esem[e] or seen.get(S, 0) >= v:
                            continue
                        seen[S] = v
                        waits.append((S, v))
                    out.append((waits, None, None, 0))
                    s += 1
                    continue
                waits = []
                for d, raw in o.deps.items():
                    dop = self.ops[d]
                    S, v = dop.ev
                    if dop.eng == e and dop.dma_sem is None:
                        if e == "pe" or not raw:
                            continue
                    if seen.get(S, 0) >= v:
                        continue
                    seen[S] = v
                    waits.append((S, v))
                S, v = o.ev
                out.append((waits, o.fn, S, S.step))
            prog[e] = out
        with nc.Block() as block:
            def run(lst):
                def body(e):
                    for waits, fn, sem, inc in lst:
                        for S, v in waits:
                            e.wait_ge(S.h, v)
                        if fn is not None:
                            ins = fn(e)
                            ins.then_inc(sem.h, inc)
                return body
            block.tensor(run(prog["pe"]))
            block.scalar(run(prog["act"]))
            block.vector(run(prog["dve"]))
            block.gpsimd(run(prog["pool"]))
            block.sync(run(prog["sp"]))


def build_nc():
    nc = bass.Bass("TRN2", target_bir_lowering=False)

    def din(name, shape, dt=F32):
        return nc.dram_tensor(name, list(shape), dt, kind="ExternalInput").ap()

    def dout(name, shape):
        return nc.dram_tensor(name, list(shape), F32, kind="ExternalOutput").ap()

    xp = din("xp", [4096, 1024]); pp = din("pp", [4096, 256])
    xs = din("xs", [128, 1024]); ps_ = din("ps", [128, 256])
    ck = din("ck", [16, 2048, 512]); cv = din("cv", [16, 2048, 512])
    w_in = din("w_in", [1024, 3584]); w_out = din("w_out", [1024, 1024])
    w_gate = din("w_gate", [1024, 1024]); w_ple = din("w_ple", [256, 1024])
    gn_col = din("gn_col", [128, 8])
    wsT_d = din("wsT", [4, 128, 128]); wsS_d = din("wsS", [4, 128, 128])
    bcol_d = din("bcol", [128, 8])
    gvec = din("gvec", [4, 512])
    gple_d = din("gple", [1, 1024])
    gcol_d = din("gcol", [128, 1])
    gocol_d = din("gocol", [128, 8])
    ident_d = din("ident", [128, 128])
    masks_d = din("masks", [5, 128, 128])
    maskc_d = din("maskc", [128, 10 * 64])
    maskn_d = din("maskn", [128, 1024])

    yp = dout("yp", [4096, 1024]); ys = dout("ys", [128, 1024])
    kw = dout("kw", [4096, 512]); vw = dout("vw", [4096, 512])
    kn = dout("kn", [128, 512]); vn_o = dout("vn", [128, 512]); vac = dout("vac", [128, 512])
    vscr = nc.dram_tensor("vscr", [4096, 520], BF16, kind="Internal").ap()

    with ExitStack() as st:
        P = Prog(nc, st)

        def edur(eng, out, *ins):
            n = out.free_size()
            if eng == "act":
                return 0.2 + n / 1000.0
            if eng == "pool":
                return 0.5 + n / 350.0
            fast = out.dtype == BF16 and all(getattr(i, "dtype", BF16) == BF16 for i in ins)
            return 0.1 + n / (1900.0 if fast else 940.0)

        def ACT(out, in_, func, R, W, scale=1.0, accum=None):
            d = edur("act", out)
            if accum is None:
                P.op("act", lambda e: e.activation(out=out, in_=in_, func=func, scale=scale), R, W, d)
            else:
                P.op("act", lambda e: e.activation(out=out, in_=in_, func=func, scale=scale, accum_out=accum), R, W, d + 0.1)

        def TT(eng, out, in0, in1, op, R, W):
            d = 1.2 if (eng == "pool" and op == ALU.pow) else edur(eng, out, in0, in1)
            P.op(eng, lambda e: e.tensor_tensor(out, in0, in1, op), R, W, d)

        def TS(eng, out, in0, s1, s2, op0, op1, R, W):
            d = edur(eng, out, in0)
            if s2 is None:
                P.op(eng, lambda e: e.tensor_scalar(out, in0, s1, None, op0), R, W, d)
            else:
                P.op(eng, lambda e: e.tensor_scalar(out, in0, s1, s2, op0, op1), R, W, d)

        def CP(eng, out, in_, R, W):
            d = edur(eng, out, in_)
            if eng == "act":
                P.op("act", lambda e: e.copy(out=out, in_=in_), R, W, d)
            else:
                P.op(eng, lambda e: e.tensor_copy(out, in_), R, W, d)

        def MEMSET(eng, ap, val, W):
            P.op(eng, lambda e: e.memset(ap, val), (), W, edur(eng, ap))

        def MMS(items, R, W):
            def fn(e):
                ins = None
                for (o, l, r, s0, s1) in items:
                    ins = e.matmul(o, l, r, start=s0, stop=s1, skip_group_check=True)
                return ins
            d = sum(0.062 + r.free_size() / 3200.0 for (o, l, r, s0, s1) in items)
            P.op("pe", fn, R, W, d)

        def TRS(items, R, W):
            def fn(e):
                ins = None
                for (o, i, idn) in items:
                    ins = e.transpose(o, i, idn)
                return ins
            d = sum((0.25 if i.dtype == F32 else 0.11) for (o, i, idn) in items)
            P.op("pe", fn, R, W, d)

        win = P.sb("win", [128, 8, 3584], BF16); Bwin = Buf("win")
        wout = P.sb("wout", [128, 8, 1024], BF16); Bwout = Buf("wout")
        wgate = P.sb("wgate", [128, 8, 1024], BF16); Bwgate = Buf("wgate")
        wple = P.sb("wple", [128, 2, 1024], BF16); Bwple = Buf("wple")
        big = P.sb("big", [128, 8192 + 16 * 520], BF16); Bbig = Buf("big")
        kT = big[:, 0:8192].rearrange("p (c t) -> p c t", c=4)
        vr4 = big[:, 8192:8192 + 16 * 520].rearrange("p (n f) -> p n f", n=16)
        BkT = [Buf("kT%d" % i) for i in range(4)]
        Bvr4 = [Buf("vr4_%d" % i) for i in range(4)]
        qT = P.sb("qT", [128, 4, 512], BF16); BqT = Buf("qT")
        vnat = P.sb("vnat", [128, 5, 520], BF16); Bvnat = [Buf("vnat%d" % i) for i in range(5)]
        aout = P.sb("aout", [128, 4, 512], BF16); Baout = [Buf("aout%d" % i) for i in range(4)]
        zbg = P.sb("zbg", [128, 4, 512], BF16); Bzbg = [Buf("zbg%d" % i) for i in range(4)]
        oacc = P.sb("oacc", [128, 8, 512], F32); Boacc = [Buf("oacc%d" % i) for i in range(8)]
        identf = P.sb("identf", [128, 128], F32); identb = P.sb("identb", [128, 128], BF16); Bid = Buf("ident")
        masks = P.sb("masks", [128, 5, 128], BF16); Bmask = Buf("masks")
        gbc = P.sb("gbc", [128, 1, 512], F32); Bgbc = Buf("gbc")
        gk64 = P.sb("gk64", [128, 64], F32)
        gple = P.sb("gple", [128, 1024], F32); Bgple = Buf("gple")
        cols = P.sb("cols", [128, 40], F32); Bcols = Buf("cols")
        wsb = P.sb("wsb", [128, 8, 128], BF16); Bws = Buf("ws")
        xinA = P.sb("xinA", [128, 1024], F32); BxinA = Buf("xinA")
        b1A = P.sb("b1A", [128, 1024], BF16); Bb1A = Buf("b1A")
        wsf = xinA[:].rearrange("p (a b) -> p a b", a=8)
        stg = [oacc[:, 2 * i:2 * i + 2, :].rearrange("p a b -> p (a b)") for i in range(4)]
        Bstg = [[Boacc[2 * i], Boacc[2 * i + 1]] for i in range(4)]
        mstage = stg[0][:, 0:640].rearrange("p (a b) -> p a b", a=5)
        b1 = P.sb("b1", [128, 1024], BF16); Bb1 = Buf("b1")
        tT = [P.sb("tT%d" % i, [128, 8, 128], BF16) for i in range(2)]; BtT = [Buf("tT%d" % i) for i in range(2)]
        wk = P.sb("wk", [128, 7, 512], F32); Bwk = [Buf("wk%d" % i) for i in range(7)]
        wc = P.sb("wc", [128, 5, 512], F32); Bwc = [Buf("wc%d" % i) for i in range(5)]
        vnb = P.sb("vnb", [128, 512], BF16); Bvnb = Buf("vnb")
        qb = vnb; Bqb = Bvnb
        kb = vnb; Bkb = Bvnb
        NPT = 6
        pT = [wk[:, 6 - i, :].bitcast(BF16) for i in range(NPT)]; BpT = [Bwk[6 - i] for i in range(NPT)]
        pin = P.sb("pin", [128, 256], F32); Bpin = Buf("pin")
        pbf = P.sb("pbf", [128, 256], BF16); Bpbf = Buf("pbf")
        pTt = P.sb("pTt", [128, 2, 128], BF16); BpTt = Buf("pTt")
        stx2 = P.sb("stx", [128, 2, 4], F32); Bstx2 = [Buf("stx0"), Buf("stx1")]
        stv = P.sb("stv", [128, 12], F32); Bstv = Buf("stv")
        stq = P.sb("stq", [128, 24], F32); Bstq = Buf("stq")
        stk = P.sb("stk", [128, 24], F32); Bstk = Buf("stk")
        sta = P.sb("sta", [128, 12], F32); Bsta = Buf("sta")
        stb = P.sb("stb", [128, 32], F32); Bstb = Buf("stb")
        ste = P.sb("ste", [128, 8], F32); Bste = Buf("ste")

        ptr = [P.ps("ptr%d" % i, [128, 1024], BF16) for i in range(2)]; Bptr = [Buf("ptr%d" % i) for i in range(2)]
        pm = [P.ps("pm%d" % i, [128, 512], F32) for i in range(4)]; Bpm = [Buf("pm%d" % i) for i in range(4)]
        pa = [P.ps("pa%d" % i, [128, 512], F32) for i in range(2)]; Bpa = [Buf("pa%d" % i) for i in range(2)]

        cnt = {"pm": 0, "ptr": 0, "stg": 0, "pT": 0, "stx": 0, "pS": 0}

        def nxt(k, n):
            v = cnt[k] % n
            cnt[k] += 1
            return v

        S_x = [P.new_sem("x%d" % i) for i in range(2)]
        S_c = P.new_sem("c")
        S_kc = [P.new_sem("kc%d" % i) for i in range(3)]
        S_vc = [P.new_sem("vc%d" % i) for i in range(3)]
        S_p = P.new_sem("p")
        S_vn = [P.new_sem("vn%d" % i) for i in range(5)]
        S_vr = [P.new_sem("vr%d" % i) for i in range(4)]
        S_wk = [P.new_sem("wk%d" % i) for i in range(7)]
        Bout = Buf("out_dram")
        Bvscr = [Buf("vscr%d" % i) for i in range(8)]

        mhalf = cols[:, 17:32]

        Bmh = Buf("mhalf")
        P.dma("sp", cols[:, 0:8], gn_col[:, :], S_c, writes=[Bcols], nbytes=4096, par=True)
        P.dma("sp", cols[:, 8:16], bcol_d[:, :], S_c, writes=[Bcols], nbytes=4096, par=True)
        P.dma("sp", cols[:, 16:17], gcol_d[:, :], S_c, writes=[Bcols], nbytes=512, par=True)
        P.dma("sp", cols[:, 32:40], gocol_d[:, :], S_c, writes=[Bcols], nbytes=4096, par=True)
        P.dma("sp", identf[:], ident_d[:, :], S_c, writes=[Bid], par=True)
        for m in range(5):
            P.dma("sp", mstage[:, m, :], masks_d[m, :, :], S_c, writes=Bstg[0], par=True)
        P.dma("sp", gbc[:, 0, :], gvec[0:1, :].partition_broadcast(128), S_c, writes=[Bgbc], nbytes=262144, par=True)
        P.dma("sp", gk64[:], gvec[3:4, 0:64].partition_broadcast(128), S_c, writes=[Bgbc], nbytes=32768, par=True)
        P.dma("sp", gple[:], gple_d[0:1, :].partition_broadcast(128), S_c, writes=[Bgple], nbytes=524288, par=True)
        for gI in range(4):
            P.dma("sp", wsf[:, gI, :], wsT_d[gI, :, :], S_c, writes=[BxinA], par=True)
            P.dma("sp", wsf[:, 4 + gI, :], wsS_d[gI, :, :], S_c, writes=[BxinA], par=True)
        cids = [o.idx for o in P.ops if o.dma_sem is S_c]
        for Bc in (Bcols, Bid, Bgbc, Bgple, BxinA, Bstg[0][0], Bstg[0][1]):
            Bc.w = list(cids)
        MEMSET("pool", cols[:, 17:32], -0.5, [Bmh])
        CP("dve", identb[:], identf[:], [Bid], [Bid])
        CP("dve", masks[:], mstage[:], Bstg[0], [Bmask])
        P.op("dve", lambda e: e.tensor_tensor(wsb[:], wsf[:], mstage[:, 0:1, :].to_broadcast([128, 8, 128]), ALU.mult),
             [BxinA] + Bstg[0], [Bws], 1.2)
        MEMSET("pool", vnat[:], 1.0, Bvnat)

        S_stg = [S_kc[0], S_kc[1], S_kc[2], S_vc[0]]

        def load_w(dst, src, ncols, kcs, scale_off, name):
            res = []
            for c0 in range(0, ncols, 1024):
                cw = min(1024, ncols - c0)
                bl = []
                for kc in range(kcs):
                    Bd = Buf("%s_%d_%d" % (name, c0, kc))
                    bl.append(Bd)
                    xi = nxt("stg", 4)
                    P.dma("sp", stg[xi][:, 0:cw], src[kc * 128:(kc + 1) * 128, c0:c0 + cw], S_stg[xi], writes=Bstg[xi],
                          nbytes=128 * cw * 4)
                    if scale_off is not None:
                        ACT(dst[:, kc, c0:c0 + cw], stg[xi][:, 0:cw], AF.Copy, Bstg[xi] + [Bcols], [Bd],
                            scale=cols[:, scale_off + kc:scale_off + kc + 1])
                    else:
                        CP("dve", dst[:, kc, c0:c0 + cw], stg[xi][:, 0:cw], Bstg[xi], [Bd])
                res.append(bl)
            return res

        Bw_late = {}

        def load_late_weights():
            Bw_late["wout"] = load_w(wout, w_out, 1024, 8, 32, "wout")[0]
            Bw_late["wgate"] = load_w(wgate, w_gate, 1024, 8, None, "wgate")[0]
            Bw_late["wple"] = load_w(wple, w_ple, 1024, 2, None, "wple")[0]

        def rstd_calc(stat, Bstat, G, width, c_ssq, c_tmp, c_r):
            TS("dve", stat[:, c_tmp:c_tmp + G], stat[:, c_ssq:c_ssq + G], 1.0 / width, EPS, ALU.mult, ALU.add, [Bstat], [Bstat])
            TT("pool", stat[:, c_r:c_r + G], stat[:, c_tmp:c_tmp + G], mhalf[:, 0:G], ALU.pow, [Bstat, Bmh], [Bstat])

        def group_ssq(src, Bsrc, G, stat, Bstat, c_ssq, scratch, Bscr, scale=None, extraR=()):
            if scale is None:
                ACT(scratch, src, AF.Square, [Bsrc], [Bscr])
            else:
                ACT(scratch, src, AF.Square, [Bsrc] + list(extraR), [Bscr], scale=scale)
            P.op("dve", lambda e: e.tensor_reduce(out=stat[:, c_ssq:c_ssq + G],
                                                  in_=scratch.rearrange("p (g w) -> p g w", g=G),
                                                  axis=AX.X, op=ALU.add), [Bscr], [Bstat], 0.65)

        def transposes8(src_bf, Bsrc, nchunks, dst3, Bdst, scale=None):
            pi = nxt("ptr", 2)
            TRS([(ptr[pi][:, c * 128:(c + 1) * 128], src_bf[:, c * 128:(c + 1) * 128], identb[:]) for c in range(nchunks)],
                [Bsrc, Bid], [Bptr[pi]])
            src3 = ptr[pi][:, 0:nchunks * 128].rearrange("p (c t) -> p c t", c=nchunks)
            if scale is None:
                CP("dve", dst3, src3, [Bptr[pi]], [Bdst])
            else:
                ACT(dst3, src3, AF.Copy, [Bptr[pi], Bcols], [Bdst], scale=scale)

        u, Bu = wk[:, 0, :], Bwk[0]
        va, Bva = wk[:, 1, :], Bwk[1]
        za, Bza = wk[:, 2, :], Bwk[2]
        q, Bq = wk[:, 3, :], Bwk[3]
        k, Bk = wk[:, 4, :], Bwk[4]
        v, Bv = wk[:, 5, :], Bwk[5]
        sq, Bsq = wk[:, 6, :], Bwk[6]

        def A0(x_src):
            si = nxt("stx", 2)
            stx = stx2[:, si, :]; Bstx = Bstx2[si]
            P.dma("sp", xinA[:], x_src, S_x[0], writes=[BxinA], nbytes=524288)
            x_t = xinA; Bx = BxinA
            MEMSET("dve", stx, 0.0, [Bstx])
            ACT(b1A[:], x_t[:], AF.Square, [Bx, Bstx], [Bb1A, Bstx], accum=stx[:, 0:1])
            rstd_calc(stx, Bstx, 1, 1024.0, 0, 1, 2)
            ACT(b1A[:], x_t[:], AF.Copy, [Bx], [Bb1A])
            transposes8(b1A, Bb1A, 8, tT[0][:], BtT[0])
            return dict(xT=tT[0], BxT=BtT[0], stx=stx, Bstx=Bstx)

        def A1(c, dst):
            xT, BxT = c["xT"], c["BxT"]
            stx, Bstx = c["stx"], c["Bstx"]
            rx = stx[:, 2:3]

            def proj(j):
                pi = nxt("pm", 4)
                MMS([(pm[pi][:], xT[:, kc, :], win[:, kc, j * 512:(j + 1) * 512], kc == 0, kc == 7) for kc in range(8)],
                    [BxT] + Bw_late["win"][(j * 512) // 1024], [Bpm[pi]])
                return pi

            pi = proj(1)
            ACT(va, pm[pi][:], AF.Copy, [Bpm[pi], Bstx], [Bva], scale=rx)
            group_ssq(pm[pi][:], Bpm[pi], 4, stv, Bstv, 0, sq, Bsq, scale=rx, extraR=[Bstx])
            pi = proj(0)
            ACT(u, pm[pi][:], AF.Copy, [Bpm[pi], Bstx], [Bu], scale=rx)
            pi = proj(2)
            ACT(za, pm[pi][:], AF.Silu, [Bpm[pi], Bstx], [Bza], scale=rx)
            pi = proj(4)
            ACT(k, pm[pi][:], AF.Copy, [Bpm[pi], Bstx], [Bk], scale=rx)
            group_ssq(pm[pi][:], Bpm[pi], 8, stk, Bstk, 0, sq, Bsq, scale=rx, extraR=[Bstx])
            pi = proj(3)
            ACT(q, pm[pi][:], AF.Copy, [Bpm[pi], Bstx], [Bq], scale=rx)
            group_ssq(pm[pi][:], Bpm[pi], 8, stq, Bstq, 0, sq, Bsq, scale=rx, extraR=[Bstx])
            pi = proj(5)
            ACT(v, pm[pi][:], AF.Copy, [Bpm[pi], Bstx], [Bv], scale=rx)
            ACT(dst["vbf"], pm[pi][:].rearrange("p (h d) -> p h d", h=8), AF.Copy, [Bpm[pi], Bstx], [dst["Bvbf"]], scale=rx)
            P.dma("pool", dst["vout"], v, S_wk[5], reads=[Bv], nbytes=262144)
            pi = proj(6)
            ACT(dst["zb"], pm[pi][:], AF.Silu, [Bpm[pi], Bstx], [dst["Bzb"]], scale=rx)

        def A2(wsoff, bcoff, dst, sample):
            rstd_calc(stv, Bstv, 4, 128.0, 0, 4, 8)
            va3 = va.rearrange("p (g w) -> p g w", g=4)
            TT("dve", va3, va3, stv[:, 8:12].unsqueeze(2).to_broadcast([128, 4, 128]), ALU.mult, [Bva, Bstv], [Bva])
            if sample:
                TT("dve", va, va, gbc[:, 0, :], ALU.mult, [Bva, Bgbc], [Bva])
                CP("dve", vnb[:], va, [Bva], [Bvnb])
                P.dma("pool", vac[:, :], va, S_wk[1], reads=[Bva], nbytes=262144)
            else:
                TT("dve", vnb[:], va, gbc[:, 0, :], ALU.mult, [Bva, Bgbc], [Bvnb])
            pg = nxt("pm", 4)
            MMS([(pm[pg][:, g * 128:(g + 1) * 128], wsb[:, wsoff + g, :], vnb[:, g * 128:(g + 1) * 128], True, True) for g in range(4)],
                [Bws, Bvnb], [Bpm[pg]])
            rstd_calc(stk, Bstk, 8, 64.0, 0, 8, 16)
            k3 = k.rearrange("p (g w) -> p g w", g=8)
            TT("dve", k3, k3, stk[:, 16:24].unsqueeze(2).to_broadcast([128, 8, 64]), ALU.mult, [Bk, Bstk], [Bk])
            TT("dve", k3, k3, gk64[:].unsqueeze(1).to_broadcast([128, 8, 64]), ALU.mult, [Bk, Bgbc], [Bk])
            P.dma("pool", dst["kout"], k, S_wk[4], reads=[Bk], nbytes=262144)
            for g in range(4):
                sl = slice(g * 128, (g + 1) * 128)
                P.op("dve", (lambda sl=sl, g=g: (lambda e: e.scalar_tensor_tensor(
                    out=u[:, sl], in0=pm[pg][:, sl], scalar=cols[:, 8 + bcoff + g:9 + bcoff + g], in1=u[:, sl],
                    op0=ALU.add, op1=ALU.mult)))(), [Bpm[pg], Bcols, Bu], [Bu], 0.3)
            group_ssq(u, Bu, 4, sta, Bsta, 0, sq, Bsq)
            CP("act", kb[:], k, [Bk], [Bkb])
            transposes8(kb, Bkb, 4, dst["kT"], dst["BkT"])
            rstd_calc(sta, Bsta, 4, 128.0, 0, 4, 8)
            rstd_calc(stq, Bstq, 8, 64.0, 0, 8, 16)
            TT("dve", qb[:].rearrange("p (g w) -> p g w", g=8), q.rearrange("p (g w) -> p g w", g=8),
               stq[:, 16:24].unsqueeze(2).to_broadcast([128, 8, 64]), ALU.mult, [Bq, Bstq], [Bqb])
            transposes8(qb, Bqb, 4, dst["qT"], dst["BqT"], scale=cols[:, 16:17])
            u3 = u.rearrange("p (g w) -> p g w", g=4)
            TT("dve", u3, u3, sta[:, 8:12].unsqueeze(2).to_broadcast([128, 4, 128]), ALU.mult, [Bu, Bsta], [Bu])
            TT("dve", dst["aout"], u, za, ALU.mult, [Bu, Bza], [dst["Baout"]])

        def phaseA(x_src, wsoff, bcoff, dst, sample):
            c = A0(x_src)
            A1(c, dst)
            A2(wsoff, bcoff, dst, sample)

        def phaseC(x_src, p_src, y_dst, tcol, aout_ap, Baout_b, zb_ap, Bzb_b, ncolsel):
            y, By = wc[:, 0, :], Bwc[0]
            hh, Bh = wc[:, 1:3, :].rearrange("p a b -> p (a b)"), [Bwc[1], Bwc[2]]
            ee, Be = wc[:, 3:5, :].rearrange("p a b -> p (a b)"), [Bwc[3], Bwc[4]]
            P.dma("sp", hh, x_src, S_x[1], writes=Bh, nbytes=524288)
            P.dma("sp", pin[:], p_src, S_p, writes=[Bpin], nbytes=131072)
            CP("act", pbf[:], pin[:], [Bpin], [Bpbf])
            pi_t = nxt("ptr", 2)
            TRS([(ptr[pi_t][:, c * 128:(c + 1) * 128], pbf[:, c * 128:(c + 1) * 128], identb[:]) for c in range(2)],
                [Bpbf, Bid], [Bptr[pi_t]])
            CP("dve", pTt[:], ptr[pi_t][:, 0:256].rearrange("p (c t) -> p c t", c=2), [Bptr[pi_t]], [BpTt])
            MEMSET("dve", ste[:], 0.0, [Bste])
            for j in range(2):
                pi = nxt("pm", 4)
                MMS([(pm[pi][:], pTt[:, kc, :], wple[:, kc, j * 512:(j + 1) * 512], kc == 0, kc == 1) for kc in range(2)],
                    [BpTt] + Bw_late["wple"], [Bpm[pi]])
                ACT(ee[:, j * 512:(j + 1) * 512], pm[pi][:], AF.Square, [Bpm[pi], Bste], [Be[j], Bste], accum=ste[:, j:j + 1])
                CP("act", ee[:, j * 512:(j + 1) * 512], pm[pi][:], [Bpm[pi]], [Be[j]])
            TT("dve", ste[:, 2:3], ste[:, 0:1], ste[:, 1:2], ALU.add, [Bste], [Bste])
            rstd_calc(ste, Bste, 1, 1024.0, 2, 3, 4)
            for j in range(2):
                sl = slice(j * 512, (j + 1) * 512)
                P.op("dve", (lambda sl=sl: (lambda e: e.scalar_tensor_tensor(
                    out=ee[:, sl], in0=ee[:, sl], scalar=ste[:, 4:5], in1=gple[:, sl], op0=ALU.mult, op1=ALU.mult)))(),
                    [Be[j], Bste, Bgple], [Be[j]], 0.65)
            for half in range(2):
                TRS([(pa[half][:, hl * 65:(hl + 1) * 65], oacc[0:65, half * 4 + hl, tcol], identf[0:65, 0:65]) for hl in range(4)],
                    [Boacc[half * 4 + hl] for hl in range(4)] + [Bid], [Bpa[half]])
            for half in range(2):
                pa3 = pa[half][:, 0:260].rearrange("p (h f) -> p h f", h=4)
                P.op("dve", (lambda pa3=pa3, half=half: (lambda e: e.reciprocal(stb[:, half * 4:half * 4 + 4].unsqueeze(2), pa3[:, :, 64:65])))(),
                     [Bpa[half]], [Bstb], 0.15)
                TT("dve", y[:, half * 256:(half + 1) * 256].rearrange("p (h d) -> p h d", h=4), pa3[:, :, 0:64],
                   stb[:, half * 4:half * 4 + 4].unsqueeze(2).to_broadcast([128, 4, 64]), ALU.mult, [Bpa[half], Bstb], [By])
            group_ssq(y, By, 8, stb, Bstb, 8, pa[0][:], Bpa[0])
            rstd_calc(stb, Bstb, 8, 64.0, 8, 16, 24)
            y3 = y.rearrange("p (g w) -> p g w", g=8)
            TT("dve", y3, y3, stb[:, 24:32].unsqueeze(2).to_broadcast([128, 8, 64]), ALU.mult, [By, Bstb], [By])
            TT("dve", b1[:, 512:1024], y, zb_ap, ALU.mult, [By, Bzb_b], [Bb1])
            CP("act", b1[:, 0:512], aout_ap, [Baout_b], [Bb1])
            transposes8(b1, Bb1, 8, tT[1][:], BtT[1])
            mT = tT[1]; BmT = BtT[1]
            for j in range(2):
                pi = nxt("pm", 4)
                MMS([(pm[pi][:], mT[:, kc, :], wout[:, kc, j * 512:(j + 1) * 512], kc == 0, kc == 7) for kc in range(8)],
                    [BmT] + Bw_late["wout"], [Bpm[pi]])
                TT("dve", hh[:, j * 512:(j + 1) * 512], pm[pi][:], hh[:, j * 512:(j + 1) * 512], ALU.add, [Bpm[pi], Bh[j]], [Bh[j]])
            CP("act", b1[:], hh, Bh, [Bb1])
            transposes8(b1, Bb1, 8, tT[1][:], BtT[1])
            hT = tT[1]; BhT = BtT[1]
            pg_ = []
            for j in range(2):
                pi = nxt("pm", 4)
                pg_.append(pi)
                MMS([(pm[pi][:], hT[:, kc, :], wgate[:, kc, j * 512:(j + 1) * 512], kc == 0, kc == 7) for kc in range(8)],
                    [BhT] + Bw_late["wgate"], [Bpm[pi]])
                ACT(pm[pi][:], pm[pi][:], AF.Sigmoid, [Bpm[pi]], [Bpm[pi]])
            for j in range(2):
                sl = slice(j * 512, (j + 1) * 512)
                TT("dve", ee[:, sl], pm[pg_[j]][:], ee[:, sl], ALU.mult, [Be[j], Bpm[pg_[j]]], [Be[j]])
                TT("dve", ee[:, sl], hh[:, sl], ee[:, sl], ALU.add, [Bh[j], Be[j]], [Be[j]])
            P.dma("pool", y_dst, ee, S_wk[1], reads=Be, nbytes=524288)

        def mk_dst(g, tt):
            s, jb = divmod(g, 4)
            i = 4 * jb + tt
            row = s * 2048 + i * 128
            slot = i % 5
            return row, slot, dict(
                aout=aout[:, tt, :], Baout=Baout[tt],
                qT=qT[:, :, tt * 128:(tt + 1) * 128], BqT=BqT,
                kT=kT[:, :, i * 128:(i + 1) * 128], BkT=BkT[jb],
                kout=kw[row:row + 128, :], vout=vw[row:row + 128, :],
                vbf=vnat[:, slot, :].rearrange("p (h f) -> p h f", h=8)[:, :, 0:64], Bvbf=Bvnat[slot],
                zb=zbg[:, tt, :], Bzb=Bzbg[tt],
            )

        def A_tile(g, tt, ctx):
            s, jb = divmod(g, 4)
            row, slot, dst = mk_dst(g, tt)
            A1(ctx, dst)
            P.dma("pool", vscr[row:row + 128, :], vnat[:, slot, :], S_vn[slot], reads=[Bvnat[slot]], writes=[Bvscr[g]], nbytes=133120, par=True)
            nctx = None
            nrow = row + 128
            if nrow < NSEQ * 2048 and (tt < 3 or g + 1 < NG):
                nctx = A0(xp[nrow:nrow + 128, :])
            A2(0, 0, dst, False)
            if tt == 3:
                base = s * 2048 + jb * 512
                for c4 in range(4):
                    srcv = vscr[base:base + 512, :].rearrange("(m c) f -> c m f", c=4)[c4]
                    P.dma("sp", vr4[:, c4 * 4 + jb, :], srcv, S_vr[jb], reads=[Bvscr[g]], writes=[Bvr4[jb]], nbytes=133120, par=True)
            return nctx

        def C_tile(g, tt):
            s, jb = divmod(g, 4)
            row = s * 2048 + (4 * jb + tt) * 128
            phaseC(xp[row:row + 128, :], pp[row:row + 128, :], yp[row:row + 128, :],
                   slice(tt * 128, (tt + 1) * 128), aout[:, tt, :], Baout[tt], zbg[:, tt, :], Bzbg[tt], None)

        def B_group(g):
            s, jb = divmod(g, 4)
            P.mark("B g%d" % g)
            units = []
            for h in range(8):
                hp, r0 = h // 2, 64 * (h % 2)
                kTh = kT[r0:r0 + 64, hp, :]
                qTh = qT[r0:r0 + 64, hp, :]
                qcls = qTh.rearrange("p (m c) -> p c m", c=4)
                tts = [tt for tt in range(4) if 4 * jb + tt >= 1]

                def nat_unit(h=h, kTh=kTh, qTh=qTh, tts=tts):
                    st_ = {}

                    def qk():
                        pd = nxt("pm", 4)
                        MMS([(pm[pd][:, tt * 128:(tt + 1) * 128], kTh[:, (4 * jb + tt) * 128:(4 * jb + tt + 1) * 128],
                              qTh[:, tt * 128:(tt + 1) * 128], True, True) for tt in range(4)],
                            [BkT[jb], BqT], [Bpm[pd]])
                        pu = nxt("pm", 4)
                        MMS([(pm[pu][:, tt * 128:(tt + 1) * 128], kTh[:, (4 * jb + tt - 1) * 128:(4 * jb + tt) * 128],
                              qTh[:, tt * 128:(tt + 1) * 128], True, True) for tt in tts],
                            [BkT[jb], BkT[max(jb - 1, 0)], BqT], [Bpm[pu]])
                        st_["pd"], st_["pu"] = pd, pu

                    def em():
                        pd, pu = st_["pd"], st_["pu"]
                        pti = nxt("pT", NPT)
                        pTn = pT[pti]; BpTn = BpT[pti]
                        st_["pT"] = (pTn, BpTn)
                        ACT(pTn[:, 0:512], pm[pd][:], AF.Exp, [Bpm[pd]], [BpTn], scale=SCALE)
                        lo = tts[0] * 128
                        ACT(pTn[:, 512 + lo:1024], pm[pu][:, lo:512], AF.Exp, [Bpm[pu]], [BpTn], scale=SCALE)
                        TT("dve", pTn[:, 0:512].rearrange("p (a b) -> p a b", a=4), pTn[:, 0:512].rearrange("p (a b) -> p a b", a=4),
                           masks[:, 0:1, :].to_broadcast([128, 4, 128]), ALU.mult, [BpTn, Bmask], [BpTn])
                        nu = len(tts)
                        TT("dve", pTn[:, 512 + lo:1024].rearrange("p (a b) -> p a b", a=nu), pTn[:, 512 + lo:1024].rearrange("p (a b) -> p a b", a=nu),
                           masks[:, 1:2, :].to_broadcast([128, nu, 128]), ALU.mult, [BpTn, Bmask], [BpTn])

                    def pv():
                        pTn, BpTn = st_["pT"]
                        items = []
                        for tt in range(4):
                            i = 4 * jb + tt
                            has_prev = i >= 1
                            items.append((pa[0][0:65, tt * 128:(tt + 1) * 128], vnat[:, i % 5, h * 65:(h + 1) * 65],
                                          pTn[:, tt * 128:(tt + 1) * 128], True, not has_prev))
                            if has_prev:
                                items.append((pa[0][0:65, tt * 128:(tt + 1) * 128], vnat[:, (i - 1) % 5, h * 65:(h + 1) * 65],
                                              pTn[:, 512 + tt * 128:512 + (tt + 1) * 128], False, True))
                        MMS(items, [BpTn] + Bvnat, [Bpa[0]])
                    return qk, em, pv

                units.append(nat_unit())

                for jb2 in range(jb + 1):
                    def r4_unit(h=h, kTh=kTh, qcls=qcls, jb2=jb2):
                        st_ = {}

                        def qk():
                            pr = nxt("pm", 4)
                            kcls = kTh[:, jb2 * 512:(jb2 + 1) * 512].rearrange("p (m c) -> p c m", c=4)
                            MMS([(pm[pr][:, c4 * 128:(c4 + 1) * 128], kcls[:, c4, :], qcls[:, c4, :], True, True) for c4 in range(4)],
                                [BkT[jb2], BqT], [Bpm[pr]])
                            st_["pr"] = pr

                        def em():
                            pr = st_["pr"]
                            pti2 = nxt("pT", NPT)
                            pTr = pT[pti2]; BpTr = BpT[pti2]
                            st_["pT"] = (pTr, BpTr)
                            ACT(pTr[:, 0:512], pm[pr][:], AF.Exp, [Bpm[pr]], [BpTr], scale=SCALE)
                            mi = 2 + min(jb - jb2, 2)
                            TT("dve", pTr[:, 0:512].rearrange("p (a b) -> p a b", a=4), pTr[:, 0:512].rearrange("p (a b) -> p a b", a=4),
                               masks[:, mi:mi + 1, :].to_broadcast([128, 4, 128]), ALU.mult, [BpTr, Bmask], [BpTr])

                        def pv():
                            pTr, BpTr = st_["pT"]
                            MMS([(pa[1][0:65, c4 * 128:(c4 + 1) * 128], vr4[:, c4 * 4 + jb2, h * 65:(h + 1) * 65],
                                  pTr[:, c4 * 128:(c4 + 1) * 128], (jb2 == 0 and c4 == 0), jb2 == jb) for c4 in range(4)],
                                [BpTr, Bvr4[jb2]], [Bpa[1]])
                            if jb2 == jb:
                                CP("dve", oacc[0:65, h, :], pa[0][0:65, :], [Bpa[0]], [Boacc[h]])
                                TT("dve", oacc[0:65, h, :].rearrange("p (m c) -> p c m", c=4),
                                   pa[1][0:65, :].rearrange("p (c m) -> p c m", c=4),
                                   oacc[0:65, h, :].rearrange("p (m c) -> p c m", c=4), ALU.add, [Bpa[1], Boacc[h]], [Boacc[h]])
                        return qk, em, pv
                    units.append(r4_unit())

            units[0][0]()
            for n in range(len(units)):
                if n + 1 < len(units):
                    units[n + 1][0]()
                units[n][1]()
                units[n][2]()


        NG = NSEQ * NJB
        assert NJB == 4
        P.mark("A g0")
        ctx = A0(xp[0:128, :])
        Bw_late["win"] = load_w(win, w_in, 3584, 8, 0, "win")
        for tt in range(4):
            ctx = A_tile(0, tt, ctx)
        load_late_weights()
        B_group(0)
        for g in range(NG - 1):
            P.mark("CA g%d" % g)
            for tt in range(4):
                C_tile(g, tt)
                ctx = A_tile(g + 1, tt, ctx)
            B_group(g + 1)
        P.mark("sample")
        P.barrier()
        for tt in range(4):
            C_tile(NG - 1, tt)

        Bs = {n: Buf("s_" + n) for n in ("kTn", "qbd", "mc", "mn", "pN")}
        kTn = big[:, 0:512].rearrange("p (c t) -> p c t", c=4)
        qbd = big[:, 512:1536].rearrange("p (c b j) -> p c b j", c=4, b=16)
        mcb = big[:, 1536:2368]
        mnb = big[:, 2368:3392]
        NKB, NVB = 10, 14
        kcb = [big[:, 3392 + i * 512:3392 + (i + 1) * 512] for i in range(4)] + [big[:, 15832:16344], aout[:, 1, :], aout[:, 2, :], aout[:, 3, :], zbg[:, 1, :], zbg[:, 2, :]]
        Bkc = [[Buf("kc%d" % i)] for i in range(5)] + [[Baout[1]], [Baout[2]], [Baout[3]], [Bzbg[1]], [Bzbg[2]]]
        S_kb = S_vn[0:5] + [S_wk[2], S_wk[3], S_wk[6]] + [P.new_sem("kx%d" % i) for i in range(2)]
        kcT = [big[:, 6464 + i * 512:6464 + (i + 1) * 512].rearrange("p (c t) -> p c t", c=4) for i in range(2)]
        BkcT = [Buf("kcT%d" % i) for i in range(2)]
        vc = [big[:, 7488 + i * 1024:7488 + (i + 1) * 1024].bitcast(F32) for i in range(3)] + [big[:, 5440:6464].bitcast(F32), big[:, 14808:15832].bitcast(F32)]
        vc += [vnat[:, 1 + 2 * i:3 + 2 * i, :].rearrange("p a b -> p (a b)")[:, 0:1024].bitcast(F32) for i in range(2)]
        vc += [wk[:, i, :] for i in range(7)]
        Bvc = [[Buf("vc%d" % i)] for i in range(5)] + [[Bvnat[1], Bvnat[2]], [Bvnat[3], Bvnat[4]]] + [[Bwk[i]] for i in range(7)]
        S_vb = [S_vc[0], S_vc[1], S_vc[2], S_kc[0], S_kc[1], S_kc[2], S_wk[0]] + S_vr[0:4] + [P.new_sem("vx%d" % i) for i in range(3)]
        vaug = [big[:, 10560 + i * 520:10560 + (i + 1) * 520] for i in range(3)]; Bvaug = [Buf("vaug%d" % i) for i in range(3)]
        pS = [big[:, 12120 + i * 832:12120 + (i + 1) * 832] for i in range(2)]; BpS = [Buf("pS%d" % i) for i in range(2)]
        pN = big[:, 13784:14808]
        ptrf = [ptr[i][:].bitcast(F32) for i in range(2)]
        S_m = P.new_sem("m2")
        P.dma("pool", mcb[:, 0:640], maskc_d[:, :], S_m, writes=[Bs["mc"]], nbytes=327680)
        P.dma("pool", mnb, maskn_d[:, :], S_m, writes=[Bs["mn"]], nbytes=524288)
        mids = [o.idx for o in P.ops if o.dma_sem is S_m]
        Bs["mc"].w = list(mids)
        Bs["mn"].w = list(mids)
        MEMSET("dve", big[:, 512:1536], 0.0, [Bs["qbd"]])
        MEMSET("pool", big[:, 10560:10560 + 3 * 520], 1.0, Bvaug)
        MEMSET("pool", vnat[:, 0, :], 1.0, [Bvnat[0]])

        dstS = dict(aout=aout[:, 0, :], Baout=Baout[0], qT=qT[:, :, 0:128], BqT=BqT, kT=kTn, BkT=Bs["kTn"],
                    kout=kn[:, :], vout=vn_o[:, :],
                    vbf=vnat[:, 0, :].rearrange("p (h f) -> p h f", h=8)[:, :, 0:64], Bvbf=Bvnat[0],
                    zb=zbg[:, 0, :], Bzb=Bzbg[0])
        phaseA(xs[:, :], 4, 4, dstS, True)
        qs4 = qT[:, :, 0:128].rearrange("p c (b t) -> p c b t", t=8)
        CP("dve", qbd[0:64, :, :, 0:8], qs4[0:64], [BqT], [Bs["qbd"]])
        CP("dve", qbd[64:128, :, :, 8:16], qs4[64:128], [BqT], [Bs["qbd"]])
        pn = [nxt("pm", 4), nxt("pm", 4)]
        for half in range(2):
            MMS([(pm[pn[half]][:, (hp % 2) * 256:(hp % 2 + 1) * 256], kTn[:, hp, :],
                  qbd[:, hp, :, :].rearrange("p b j -> p (b j)"), True, True) for hp in (2 * half, 2 * half + 1)],
                [Bs["kTn"], Bs["qbd"]], [Bpm[pn[half]]])
            ACT(pN[:, half * 512:(half + 1) * 512], pm[pn[half]][:], AF.Exp, [Bpm[pn[half]]], [Bs["pN"]], scale=SCALE)
        TT("dve", pN, pN, mnb, ALU.mult, [Bs["pN"], Bs["mn"]], [Bs["pN"]])
        pN5 = pN.rearrange("p (c b g t) -> p c b g t", c=4, b=16, g=2)
        for half in range(2):
            MMS([(pa[half][0:65, hl * 128:(hl + 1) * 128], vnat[:, 0, (half * 4 + hl) * 65:(half * 4 + hl + 1) * 65],
                  pN5[:, (half * 4 + hl) // 2, :, (half * 4 + hl) % 2, :], True, True) for hl in range(4)],
                [Bs["pN"], Bvnat[0]], [Bpa[half]])
            CP("act", oacc[0:65, half * 4:half * 4 + 4, 0:128], pa[half][0:65, :].rearrange("p (h t) -> p h t", h=4),
               [Bpa[half]], [Boacc[half * 4 + hl] for hl in range(4)])
        NT = 10

        def tile_srcs(tl):
            if tl < 8:
                return [(slice(0, 128), slice(tl, tl + 16 * 127 + 1, 16))]
            res = []
            for r in range(4):
                st0 = WB - 512 + 8 + 4 * (tl - 8) + r
                res.append((slice(32 * r, 32 * r + 32), slice(st0, st0 + 16 * 31 + 1, 16)))
            return res

        def cache_load(queue, dst, Bdst, sem, src3, b, tl):
            srcs = tile_srcs(tl)
            for (psl, rsl) in srcs:
                P.dma(queue, dst[psl, :], src3[b, rsl, :], sem, writes=Bdst, nbytes=2048 * (psl.stop - psl.start),
                      par=(len(srcs) > 1))

        cs = {"kc": 0, "kcT": 0, "vc": 0, "vaug": 0}
        sc_banks = {}

        def kside(b):
            pi = nxt("pS", 2)
            banks = (nxt("pm", 4), nxt("pm", 4))
            for tl in range(NT):
                ki = cs["kc"] % NKB; cs["kc"] += 1
                cache_load("pool", kcb[ki], Bkc[ki], S_kb[ki], ck, b, tl)
                pti = nxt("ptr", 2)
                TRS([(ptr[pti][:, c * 128:(c + 1) * 128], kcb[ki][:, c * 128:(c + 1) * 128], identb[:]) for c in range(4)],
                    Bkc[ki] + [Bid], [Bptr[pti]])
                kti = cs["kcT"] % 2; cs["kcT"] += 1
                CP("dve" if tl % 2 == 0 else "act", kcT[kti], ptr[pti][:, 0:512].rearrange("p (c t) -> p c t", c=4), [Bptr[pti]], [BkcT[kti]])
                bk = banks[0] if tl < 8 else banks[1]
                off = (tl % 8) * 64
                MMS([(pm[bk][:, off + hp * 16:off + (hp + 1) * 16], kcT[kti][:, hp, :], qbd[:, hp, b, :], True, True) for hp in range(4)],
                    [BkcT[kti], Bs["qbd"]], [Bpm[bk]])
            ACT(pS[pi][:, 0:512], pm[banks[0]][:], AF.Exp, [Bpm[banks[0]]], [BpS[pi]], scale=SCALE)
            ACT(pS[pi][:, 512:640], pm[banks[1]][:, 0:128], AF.Exp, [Bpm[banks[1]]], [BpS[pi]], scale=SCALE)
            TT("dve", pS[pi][:, 0:640], pS[pi][:, 0:640], mcb[:, 0:640], ALU.mult, [BpS[pi], Bs["mc"]], [BpS[pi]])
            return pi

        def vside(b, pi):
            for tl in range(NT):
                vi = cs["vc"] % NVB; cs["vc"] += 1
                cache_load("sp", vc[vi], Bvc[vi], S_vb[vi], cv, b, tl)
                ai = cs["vaug"] % 3; cs["vaug"] += 1
                CP(("act", "dve")[tl % 2], vaug[ai].rearrange("p (h f) -> p h f", h=8)[:, :, 0:64], vc[vi].rearrange("p (h d) -> p h d", h=8),
                   Bvc[vi], [Bvaug[ai]])
                MMS([(pa[h // 4][0:65, (h % 4) * 128 + 8 * b:(h % 4) * 128 + 8 * b + 8], vaug[ai][:, h * 65:(h + 1) * 65],
                      pS[pi][:, tl * 64 + h * 8:tl * 64 + h * 8 + 8], (b == 0 and tl == 0 and h % 4 == 0), tl == NT - 1) for h in range(8)],
                    [Bvaug[ai], BpS[pi]], [Bpa[0], Bpa[1]])

        nb = NB_S
        pis = {0: kside(0)}
        for b in range(nb):
            if b + 1 < nb:
                pis[b + 1] = kside(b + 1)
            vside(b, pis[b])
        for half in range(2):
            o3 = oacc[0:65, half * 4:half * 4 + 4, 0:128]
            TT("dve", o3, pa[half][0:65, :].rearrange("p (h t) -> p h t", h=4), o3, ALU.add,
               [Bpa[half]] + [Boacc[half * 4 + hl] for hl in range(4)], [Boacc[half * 4 + hl] for hl in range(4)])
        phaseC(xs[:, :], ps_[:, :], ys[:, :], slice(0, 128), aout[:, 0, :], Baout[0], zbg[:, 0, :], Bzbg[0], None)

        P.barrier()
        P.emit()
    return nc


def _consts():
    ki = np.arange(128)[:, None]
    qi = np.arange(128)[None, :]
    D = (ki <= qi).astype(np.float32)
    U = (ki >= qi).astype(np.float32)
    m4 = ((qi - ki) % 4 == 0).astype(np.float32)
    R0 = D * (1.0 + m4)
    R1 = U + m4
    R2 = m4
    masks = np.stack([D, U, R0, R1, R2]).astype(np.float32)
    mc = np.zeros((128, 10, 8, 8), np.float32)
    p = np.arange(128)
    for tl in range(10):
        if tl < 8:
            row = tl + 16 * p
        else:
            row = WB - 512 + 8 + 4 * (tl - 8) + (p // 32) + 16 * (p % 32)
        for t in range(8):
            mult = ((row % 16 == t % 16).astype(np.float32)
                    + ((row % 4 == t % 4) & (row >= WB - 512 + t)).astype(np.float32)
                    + (row >= WB - 128 + t).astype(np.float32))
            mc[:, tl, :, t] = mult[:, None]
    mn = np.zeros((16, 8, 4, 16, 2, 8), np.float32)
    for b in range(16):
        for t2 in range(8):
            for t in range(8):
                d = t - t2
                if d >= 0:
                    mn[b, t2, :, b, :, t] = 1.0 + (1.0 if d % 4 == 0 else 0.0) + (1.0 if d == 0 else 0.0)
    return masks, mc.reshape(128, 10 * 64), mn.reshape(128, 1024)


_NC_CACHE = {}


def kernel(x_prompt, x_sample, cache_k, cache_v, p_prompt, p_sample, g_norm, w_in, w_s, b_s,
           g_va, g_oa, g_q, g_k, g_ob, w_out, w_ple, g_ple, w_ple_gate):
    f = lambda a: np.ascontiguousarray(np.asarray(a, dtype=np.float32))
    x_prompt = f(x_prompt); x_sample = f(x_sample); cache_k = f(cache_k); cache_v = f(cache_v)
    p_prompt = f(p_prompt); p_sample = f(p_sample)
    masks, maskc, maskn = _consts()
    ws = f(w_s)[0]
    wsT = np.ascontiguousarray(ws.transpose(0, 2, 1))
    wsS = np.zeros((4, 128, 128), np.float32)
    for b in range(16):
        wsS[:, 8 * b:8 * b + 8, 8 * b:8 * b + 8] = wsT[:, 0:8, 0:8]
    bs = f(b_s)[0]
    bcol = np.zeros((128, 8), np.float32)
    bcol[:, 0:4] = bs.T
    bcol[:, 4:8] = np.tile(bs[:, 0:8].T, (16, 1))
    gvec = np.stack([f(g_va)[0].reshape(512), f(g_oa)[0].reshape(512), f(g_ob)[0].reshape(512),
                     np.tile(f(g_k)[0], 8)]).astype(np.float32)
    shared = dict(
        w_in=f(w_in)[0], w_out=f(w_out)[0], w_gate=f(w_ple_gate)[0], w_ple=f(w_ple)[0],
        gn_col=np.ascontiguousarray(f(g_norm)[0].reshape(8, 128).T),
        wsT=wsT, wsS=wsS, bcol=bcol, gvec=np.ascontiguousarray(gvec),
        gple=f(g_ple)[0].reshape(1, 1024), gcol=np.tile(f(g_q)[0], 2).reshape(128, 1).astype(np.float32),
        gocol=np.ascontiguousarray(np.concatenate([f(g_oa)[0].reshape(512), f(g_ob)[0].reshape(512)]).reshape(8, 128).T),
        ident=np.eye(128, dtype=np.float32), masks=masks, maskc=maskc, maskn=maskn,
    )
    in_maps = []
    for c in range(NCORES):
        d = dict(shared)
        d["xp"] = x_prompt[2 * c:2 * c + 2].reshape(4096, 1024)
        d["pp"] = p_prompt[0, 2 * c:2 * c + 2].reshape(4096, 256)
        d["xs"] = x_sample[16 * c:16 * c + 16].reshape(128, 1024)
        d["ps"] = p_sample[0, 16 * c:16 * c + 16].reshape(128, 256)
        d["ck"] = cache_k[0, 16 * c:16 * c + 16].reshape(16, 2048, 512)
        d["cv"] = cache_v[0, 16 * c:16 * c + 16].reshape(16, 2048, 512)
        in_maps.append(d)
    if "nc" not in _NC_CACHE:
        _NC_CACHE["nc"] = build_nc()
    nc = _NC_CACHE["nc"]
    res = run_bass_kernel_spmd(nc, in_maps, core_ids=list(range(NCORES)))
    R = res.results
    y_p = np.concatenate([R[c]["yp"].reshape(2, 2048, 1024) for c in range(NCORES)], 0)
    y_s = np.concatenate([R[c]["ys"].reshape(16, 8, 1024) for c in range(NCORES)], 0)
    kwp = np.concatenate([R[c]["kw"].reshape(2, 2048, 8, 64) for c in range(NCORES)], 0)[None]
    vwp = np.concatenate([R[c]["vw"].reshape(2, 2048, 8, 64) for c in range(NCORES)], 0)[None]
    kns = np.concatenate([R[c]["kn"].reshape(16, 8, 8, 64) for c in range(NCORES)], 0)[None]
    vns = np.concatenate([R[c]["vn"].reshape(16, 8, 8, 64) for c in range(NCORES)], 0)[None]
    vas = np.concatenate([R[c]["vac"].reshape(16, 8, 512) for c in range(NCORES)], 0)[None]
    return (y_p.astype(np.float32), y_s.astype(np.float32), kwp.astype(np.float32), vwp.astype(np.float32),
            kns.astype(np.float32), vns.astype(np.float32), vas.astype(np.float32))
```
